# Optimizing a Trainium2 kernel written in Bass

```python
import jax, jax.numpy as jnp
from jax import lax
import numpy as np

D_MODEL = 2048
BATCH = 4
SEQ = 2048
DEPTH = 2
DEC_BATCH = 32
DEC_SEQ = 16
PAST_LEN = 4096

CHUNK = 64
D_MIX = 2 * D_MODEL
D_SSD = D_MIX // 2
SSD_HEAD_DIM = 64
SSD_HEADS = D_SSD // SSD_HEAD_DIM
SSD_GROUPS = 8
SSD_HEADS_PER_GROUP = SSD_HEADS // SSD_GROUPS
D_STATE = 128
SSD_CONV = 4
D_XBC = D_SSD + 2 * SSD_GROUPS * D_STATE
D_GMLP = D_MIX - D_SSD
GMLP_GROUPS = 8
GMLP_GROUP_DIM = D_GMLP // GMLP_GROUPS
GMLP_CHUNK = 128
D_IN = D_SSD + D_XBC + SSD_HEADS + 2 * D_GMLP
D_FF = 5632
FFN_CONV = 3
EPS = 1e-6

kernel_name = "hybrid_ssd_gmlp_convffn_stream_step"


def _rmsnorm(x, w):
    xf = x.astype(jnp.float32)
    y = xf * lax.rsqrt(jnp.mean(xf * xf, axis=-1, keepdims=True) + EPS)
    return (y * w.astype(jnp.float32)).astype(x.dtype)


def _group_rmsnorm(x, w, groups):
    shp = x.shape
    xf = x.astype(jnp.float32).reshape(shp[:-1] + (groups, shp[-1] // groups))
    y = xf * lax.rsqrt(jnp.mean(xf * xf, axis=-1, keepdims=True) + EPS)
    return y.reshape(shp) * w.astype(jnp.float32)


def _causal_dwconv(x, buf, w, b):
    k = w.shape[0]
    L = x.shape[1]
    xp = jnp.concatenate([buf.astype(x.dtype), x], axis=1)
    y = xp[:, 0:L] * w[0] + b
    for i in range(1, k):
        y = y + xp[:, i:i + L] * w[i]
    return y, xp[:, L:]


def _ssd_scan(x, dt, a, bm, cm, h0, block):
    bsz, L = x.shape[0], x.shape[1]
    nc = L // block
    G, E, P, N = SSD_GROUPS, SSD_HEADS_PER_GROUP, SSD_HEAD_DIM, D_STATE
    xc = x.reshape(bsz, nc, block, G, E, P)
    dtc = dt.reshape(bsz, nc, block, G, E)
    bc = bm.reshape(bsz, nc, block, G, N)
    cc = cm.reshape(bsz, nc, block, G, N)
    acs = jnp.cumsum(dtc * a.reshape(G, E), axis=2)
    xdt = xc * dtc[..., None]
    causal = jnp.tril(jnp.ones((block, block), dtype=bool))
    seg = acs[:, :, :, None] - acs[:, :, None, :]
    decay = jnp.exp(jnp.where(causal[:, :, None, None], seg, -jnp.inf))
    cb = jnp.einsum('bcign,bcjgn->bcijg', cc, bc)
    y_diag = jnp.einsum('bcijge,bcjgep->bcigep', cb[..., None] * decay, xdt)
    decay_end = jnp.exp(acs[:, :, -1:] - acs)
    states = jnp.einsum('bcjgn,bcjgep->bcgepn', bc, xdt * decay_end[..., None])
    chunk_decay = jnp.exp(acs[:, :, -1])

    def step(h, inp):
        s, d = inp
        return d[..., None, None] * h + s, h

    h_final, h_prev = lax.scan(step, h0.reshape(bsz, G, E, P, N),
                               (jnp.moveaxis(states, 1, 0), jnp.moveaxis(chunk_decay, 1, 0)))
    h_prev = jnp.moveaxis(h_prev, 0, 1)
    y_off = jnp.einsum('bcign,bcgepn->bcigep', cc, h_prev) * jnp.exp(acs)[..., None]
    y = (y_diag + y_off).reshape(bsz, L, SSD_HEADS, P)
    return y, h_final.reshape(bsz, SSD_HEADS, P, N)


def _spatial_gate(u, v_n, w_s, b_s):
    bsz, L, _ = v_n.shape
    blk = min(GMLP_CHUNK, L)
    nc = L // blk
    vc = v_n.reshape(bsz, nc, blk, GMLP_GROUPS, GMLP_GROUP_DIM)
    pos = jnp.arange(blk)
    mask = (pos[None, :] // CHUNK) <= (pos[:, None] // CHUNK)
    w = jnp.where(mask[None], w_s[:, :blk, :blk], 0)
    s = jnp.einsum('gij,bcjgd->bcigd', w, vc) + b_s[:, :blk].T[None, None, :, :, None]
    return u * s.reshape(bsz, L, D_GMLP)


def _trunk_layer(x, conv_buf, h0, ffn_buf, norm1_w, w_in, ssd_conv_w, ssd_conv_b, dt_bias, a_log,
                 ssd_d, ssd_norm_w, gmlp_norm_w, gmlp_w_s, gmlp_b_s, w_out, norm2_w, w_up,
                 ffn_conv_w, ffn_conv_b, w_down):
    bsz, L, _ = x.shape
    f32 = jnp.float32
    h = _rmsnorm(x, norm1_w)
    proj = jnp.einsum('bld,de->ble', h, w_in)
    s1 = D_SSD
    s2 = s1 + D_XBC
    s3 = s2 + SSD_HEADS
    s4 = s3 + D_GMLP
    z, xbc, dt_raw, u, v = proj[..., :s1], proj[..., s1:s2], proj[..., s2:s3], proj[..., s3:s4], proj[..., s4:]
    xbc, new_conv = _causal_dwconv(xbc, conv_buf, ssd_conv_w, ssd_conv_b)
    xbc = jax.nn.silu(xbc).astype(f32)
    xs = xbc[..., :D_SSD].reshape(bsz, L, SSD_HEADS, SSD_HEAD_DIM)
    bm = xbc[..., D_SSD:D_SSD + SSD_GROUPS * D_STATE].reshape(bsz, L, SSD_GROUPS, D_STATE)
    cm = xbc[..., D_SSD + SSD_GROUPS * D_STATE:].reshape(bsz, L, SSD_GROUPS, D_STATE)
    dt = jax.nn.softplus(dt_raw.astype(f32) + dt_bias.astype(f32))
    a = -jnp.exp(a_log.astype(f32))
    y, h_new = _ssd_scan(xs, dt, a, bm, cm, h0.astype(f32), min(CHUNK, L))
    y = y + ssd_d.astype(f32)[:, None] * xs
    y = y.reshape(bsz, L, D_SSD) * jax.nn.silu(z.astype(f32))
    y_ssd = _group_rmsnorm(y, ssd_norm_w, SSD_GROUPS).astype(x.dtype)
    u = jax.nn.gelu(u)
    v_n = _group_rmsnorm(jax.nn.gelu(v), gmlp_norm_w, GMLP_GROUPS).astype(x.dtype)
    y_gmlp = _spatial_gate(u, v_n, gmlp_w_s, gmlp_b_s).astype(x.dtype)
    mix = jnp.concatenate([y_ssd, y_gmlp], axis=-1)
    x = x + jnp.einsum('ble,ed->bld', mix, w_out)
    h = _rmsnorm(x, norm2_w)
    up = jnp.einsum('bld,df->blf', h, w_up)
    up, new_ffn = _causal_dwconv(up, ffn_buf, ffn_conv_w, ffn_conv_b)
    g, val = up[..., :D_FF], up[..., D_FF:]
    x = x + jnp.einsum('blf,fd->bld', jax.nn.silu(g) * val, w_down)
    return x, new_conv, h_new, new_ffn, v_n


def setup_inputs(seed: int = 0) -> dict:
    key = jax.random.key(seed)
    ks = jax.random.split(key, 24)
    f32 = jnp.float32
    nrm = lambda k, shp, s: jax.random.normal(k, shp, f32) * s
    dt0 = jnp.exp(jax.random.uniform(ks[8], (DEPTH, SSD_HEADS), f32, np.log(1e-3), np.log(1e-1)))
    return {
        'x_prompt': nrm(ks[0], (BATCH, SEQ, D_MODEL), 1.0),
        'x_sample': nrm(ks[1], (DEC_BATCH, DEC_SEQ, D_MODEL), 1.0),
        'state_ssd_conv': nrm(ks[2], (DEPTH, DEC_BATCH, SSD_CONV - 1, D_XBC), 1.0),
        'state_ssd': nrm(ks[3], (DEPTH, DEC_BATCH, SSD_HEADS, SSD_HEAD_DIM, D_STATE), 0.1),
        'state_ffn_conv': nrm(ks[4], (DEPTH, DEC_BATCH, FFN_CONV - 1, 2 * D_FF), 1.0),
        'norm1_w': 1.0 + nrm(ks[5], (DEPTH, D_MODEL), 0.02),
        'w_in': nrm(ks[6], (DEPTH, D_MODEL, D_IN), D_MODEL ** -0.5),
        'ssd_conv_w': nrm(ks[7], (DEPTH, SSD_CONV, D_XBC), 0.5),
        'ssd_conv_b': nrm(ks[9], (DEPTH, D_XBC), 0.02),
        'dt_bias': dt0 + jnp.log(-jnp.expm1(-dt0)),
        'a_log': jnp.log(jax.random.uniform(ks[10], (DEPTH, SSD_HEADS), f32, 1.0, 16.0)),
        'ssd_d': 1.0 + nrm(ks[11], (DEPTH, SSD_HEADS), 0.1),
        'ssd_norm_w': 1.0 + nrm(ks[12], (DEPTH, D_SSD), 0.02),
        'gmlp_norm_w': 1.0 + nrm(ks[13], (DEPTH, D_GMLP), 0.02),
        'gmlp_w_s': nrm(ks[14], (DEPTH, GMLP_GROUPS, GMLP_CHUNK, GMLP_CHUNK), GMLP_CHUNK ** -0.5),
        'gmlp_b_s': 1.0 + nrm(ks[15], (DEPTH, GMLP_GROUPS, GMLP_CHUNK), 0.1),
        'w_out': nrm(ks[16], (DEPTH, D_MIX, D_MODEL), D_MIX ** -0.5),
        'norm2_w': 1.0 + nrm(ks[17], (DEPTH, D_MODEL), 0.02),
        'w_up': nrm(ks[18], (DEPTH, D_MODEL, 2 * D_FF), D_MODEL ** -0.5),
        'ffn_conv_w': nrm(ks[19], (DEPTH, FFN_CONV, 2 * D_FF), 0.6),
        'ffn_conv_b': nrm(ks[20], (DEPTH, 2 * D_FF), 0.02),
        'w_down': nrm(ks[21], (DEPTH, D_FF, D_MODEL), D_FF ** -0.5),
        'final_norm_w': 1.0 + nrm(ks[22], (D_MODEL,), 0.02),
    }


def reference(x_prompt, x_sample, state_ssd_conv, state_ssd, state_ffn_conv, norm1_w, w_in, ssd_conv_w,
              ssd_conv_b, dt_bias, a_log, ssd_d, ssd_norm_w, gmlp_norm_w, gmlp_w_s, gmlp_b_s, w_out,
              norm2_w, w_up, ffn_conv_w, ffn_conv_b, w_down, final_norm_w):
    bp = x_prompt.shape[0]
    yp, ys = x_prompt, x_sample
    p_conv, p_ssd, p_ffn = [], [], []
    s_conv, s_ssd, s_ffn, s_v = [], [], [], []
    for l in range(DEPTH):
        lw = (norm1_w[l], w_in[l], ssd_conv_w[l], ssd_conv_b[l], dt_bias[l], a_log[l], ssd_d[l],
              ssd_norm_w[l], gmlp_norm_w[l], gmlp_w_s[l], gmlp_b_s[l], w_out[l], norm2_w[l], w_up[l],
              ffn_conv_w[l], ffn_conv_b[l], w_down[l])
        yp, c, hs, f, _ = _trunk_layer(
            yp,
            jnp.zeros((bp, SSD_CONV - 1, D_XBC), yp.dtype),
            jnp.zeros((bp, SSD_HEADS, SSD_HEAD_DIM, D_STATE), jnp.float32),
            jnp.zeros((bp, FFN_CONV - 1, 2 * D_FF), yp.dtype),
            *lw)
        p_conv.append(c)
        p_ssd.append(hs)
        p_ffn.append(f)
        ys, c, hs, f, vn = _trunk_layer(ys, state_ssd_conv[l], state_ssd[l], state_ffn_conv[l], *lw)
        s_conv.append(c)
        s_ssd.append(hs)
        s_ffn.append(f)
        s_v.append(vn)
    yp = _rmsnorm(yp, final_norm_w)
    ys = _rmsnorm(ys, final_norm_w)
    return (yp, ys, jnp.stack(p_conv), jnp.stack(p_ssd), jnp.stack(p_ffn),
            jnp.stack(s_conv), jnp.stack(s_ssd), jnp.stack(s_ffn), jnp.stack(s_v))
```

```python
import contextlib
import numpy as np
import concourse.bass as bass
import concourse.mybir as mybir
from concourse.bass_utils import run_bass_kernel_spmd

F32 = mybir.dt.float32
BF16 = mybir.dt.bfloat16
AF = mybir.ActivationFunctionType
ALU = mybir.AluOpType

D = 2048
KC = 16
NL = 2
SEQ = 2048
PB = 512
NPASS = SEQ // PB
NSQ = 4
ST = 16
NSAMP = NSQ * ST
NTMAX = PB + NSAMP
SAMPLE_PASS = 0
DIN = 10272
DFF = 5632
NWB = 4
EPS = 1e-6
CVL = 592
import os
MK_STAGE = int(os.environ.get('MK_STAGE', '99'))
MK_NPASS = int(os.environ.get('MK_NPASS', str(NPASS)))
MK_NL = int(os.environ.get('MK_NL', str(NL)))
MK_SUB = int(os.environ.get('MK_SUB', '99'))
MK_CORES = int(os.environ.get('MK_CORES', '8'))
MK_SMALLW = int(os.environ.get('MK_SMALLW', '0'))


class _Stop(Exception):
    pass


class Op:
    __slots__ = ("eng", "fn", "deps", "flag", "cnt", "sem", "isdma")

    def __init__(self, eng, fn, isdma):
        self.eng = eng
        self.fn = fn
        self.deps = []
        self.flag = False
        self.cnt = 0
        self.sem = None
        self.isdma = isdma


class Sched:
    ENGS = ("pe", "act", "dve", "pool", "sp")
    NDMA = {"sp": 12, "pool": 6}

    def __init__(self, nc):
        self.nc = nc
        self.ops = {e: [] for e in self.ENGS}
        self.lastw = {}
        self.readers = {}
        self.ndma = {e: 0 for e in self.ENGS}
        self.out_dmas = []

    def add(self, eng, fn, reads=(), writes=(), dma=False):
        op = Op(eng, fn, dma)
        deps = {}

        def need(p, raw):
            if p is op:
                return
            if (not p.isdma) and (not dma) and p.eng == eng:
                if eng == "pe" or not raw:
                    return
            deps[id(p)] = p

        for k in reads:
            w = self.lastw.get(k)
            if w is not None:
                need(w, True)
            if isinstance(k, tuple) and k[0] == "ps":
                rs = self.readers.get(k)
                if rs:
                    for r in rs.values():
                        if not isinstance(r, list):
                            need(r, False)
        for k in writes:
            w = self.lastw.get(k)
            if w is not None:
                need(w, False)
            rs = self.readers.get(k)
            if rs:
                for r in rs.values():
                    if isinstance(r, list):
                        for rr in r:
                            need(rr, False)
                    else:
                        need(r, False)
        for k in writes:
            self.lastw[k] = op
            self.readers[k] = {}
        for k in reads:
            rs = self.readers.setdefault(k, {})
            if dma:
                rs.setdefault("dma", []).append(op)
            else:
                rs[eng] = op
        op.deps = list(deps.values())
        for p in op.deps:
            p.flag = True
        if dma:
            op.flag = True
            i = self.ndma[eng]
            self.ndma[eng] = i + 1
            R = self.NDMA[eng]
            op.sem = (eng, i % R)
            op.cnt = 16 * (i // R + 1)
        self.ops[eng].append(op)
        return op

    def dma(self, out, in_, reads=(), writes=(), q="sp", is_out=False):
        op = self.add(q, lambda e: e.dma_start(out=out, in_=in_), reads, writes, dma=True)
        if is_out:
            self.out_dmas.append(op)
        return op

    def emit(self):
        nc = self.nc
        for e in self.ENGS:
            c = 0
            for op in self.ops[e]:
                if not op.isdma and op.flag:
                    c += 1
                    op.cnt = c
        with contextlib.ExitStack() as st:
            csem = {e: st.enter_context(nc.semaphore(f"c_{e}")) for e in ("pe", "act", "dve", "pool")}
            dsem = {}
            for e, R in self.NDMA.items():
                for i in range(R):
                    dsem[(e, i)] = st.enter_context(nc.semaphore(f"d_{e}{i}"))
            block = st.enter_context(nc.Block())

            def run(e, eo):
                waited = {}

                def wait(sem_key, sem, cnt):
                    if waited.get(sem_key, 0) >= cnt:
                        return
                    waited[sem_key] = cnt
                    eo.wait_ge(sem, cnt)

                for op in self.ops[e]:
                    for p in op.deps:
                        if p.isdma:
                            wait(p.sem, dsem[p.sem], p.cnt)
                        else:
                            wait(p.eng, csem[p.eng], p.cnt)
                    if op.isdma:
                        if op.cnt > 16:
                            wait(op.sem, dsem[op.sem], op.cnt - 16)
                        op.fn(eo).then_inc(dsem[op.sem], 16)
                    else:
                        ins = op.fn(eo)
                        if op.flag:
                            ins.then_inc(csem[e], 1)
                if e == "sp":
                    for op in self.out_dmas:
                        wait(op.sem, dsem[op.sem], op.cnt)

            @block.tensor
            def _(pe):
                run("pe", pe)

            @block.scalar
            def _(act):
                run("act", act)

            @block.vector
            def _(dve):
                run("dve", dve)

            @block.gpsimd
            def _(pool):
                run("pool", pool)

            @block.sync
            def _(sp):
                run("sp", sp)


class Rot:
    def __init__(self, tiles, name):
        self.t = tiles
        self.i = 0
        self.name = name

    def get(self):
        i = self.i % len(self.t)
        self.i += 1
        return self.t[i], (self.name, i)


def build_program():
    nc = bass.Bass("TRN2", target_bir_lowering=False)

    def din(name, shape):
        return nc.dram_tensor(name, shape, F32, kind="ExternalInput").ap()

    def dout(name, shape):
        return nc.dram_tensor(name, shape, F32, kind="ExternalOutput").ap()

    xT = din("xT", [D, SEQ])
    xsT = din("xsT", [D, NSAMP])
    cvec = din("cvec", [128, 2 * CVL + 16])
    alog = din("alog", [NL, 32])
    dtb = din("dtb", [32, NL])
    bs = din("bs", [NL, 1024])
    bss = din("bss", [NL, 128])
    wsT = din("wsT", [NL, 128, 1024])
    w_in = din("w_in", [NL, D, DIN])
    if MK_SMALLW:
        w_out = din("w_out", [NL, 128, 128])
        w_up = din("w_up", [NL, 128, 128])
        w_down = din("w_down", [NL, 128, 128])
    else:
        w_out = din("w_out", [NL, 4096, D])
        w_up = din("w_up", [NL, D, 2 * DFF])
        w_down = din("w_down", [NL, DFF, D])
    sconv = din("sconv", [NL, 32, 128, NSQ * 3])
    sssd = din("sssd", [NL, NSQ, 128, 2048])
    sffn = din("sffn", [NL, 88, 128, NSQ * 2])

    yT = dout("yT", [D, SEQ])
    ysT = dout("ysT", [D, NSAMP])
    o_pconv = dout("o_pconv", [NL, 32, 128, 3])
    o_pssd = dout("o_pssd", [NL, 128, 2048])
    o_pffn = dout("o_pffn", [NL, 88, 128, 2])
    o_sconv = dout("o_sconv", [NL, 32, 128, NSQ * 3])
    o_sssd = dout("o_sssd", [NL, NSQ, 128, 2048])
    o_sffn = dout("o_sffn", [NL, 88, 128, NSQ * 2])
    o_sv = dout("o_sv", [NL, 16, 128, NSAMP])

    with contextlib.ExitStack() as st:

        def sb(name, shape, dt=F32):
            return st.enter_context(nc.sbuf_tensor(name, shape, dt))

        S = Sched(nc)

        def act(out, in_, func, reads, writes, bias=None, scale=None):
            kw = {}
            if bias is not None:
                kw["bias"] = bias
            if scale is not None:
                kw["scale"] = scale
            S.add("act", lambda e: e.activation(out=out, in_=in_, func=func, **kw), reads, writes)

        def mm(out, lhsT, rhs, start, stop, reads, writes):
            S.add("pe", lambda e: e.matmul(out, lhsT=lhsT, rhs=rhs, start=start, stop=stop), reads, writes)

        def tr(out, in_, ident, reads, writes):
            S.add("pe", lambda e: e.transpose(out=out, in_=in_, identity=ident), reads, writes)

        def tt(out, in0, in1, op, reads, writes, eng="dve"):
            S.add(eng, lambda e: e.tensor_tensor(out=out, in0=in0, in1=in1, op=op), reads, writes)

        def stt(out, in0, scalar, in1, op0, op1, reads, writes):
            S.add("dve", lambda e: e.scalar_tensor_tensor(out=out, in0=in0, scalar=scalar, in1=in1, op0=op0, op1=op1),
                  reads, writes)

        def ts1(out, in0, s1, op0, reads, writes, eng="dve"):
            S.add(eng, lambda e: e.tensor_scalar(out=out, in0=in0, scalar1=s1, scalar2=None, op0=op0), reads, writes)

        def cp(out, in_, reads, writes, eng="dve"):
            S.add(eng, lambda e: e.tensor_copy(out=out, in_=in_), reads, writes)

        def memset(ap, val, writes, eng="dve"):
            S.add(eng, lambda e: e.memset(ap, val), (), writes)

        def recip(out, in_, reads, writes):
            S.add("dve", lambda e: e.reciprocal(out=out, in_=in_), reads, writes)

        def asel(out, in_, pattern, cmp, fill, base, cm, reads, writes):
            S.add("pool", lambda e: e.affine_select(out=out, in_=in_, pattern=pattern, compare_op=cmp, fill=fill,
                                                    base=base, channel_multiplier=cm), reads, writes)

        x = sb("x", [128, KC, NTMAX])
        hT = sb("hT", [128, KC, NTMAX], BF16)
        mix = sb("mix", [128, 16, NTMAX], BF16)
        wbs = [sb(f"wb{i}", [128, 4096], BF16) for i in range(NWB)]
        wrot = Rot(wbs, "wb")
        Hst = sb("Hst", [128, NL, 8, 256])
        halS = sb("halS", [128, NL, 32, 3])
        halF = sb("halF", [128, NL, 88, 2])
        cv = sb("cv", [128, 2 * CVL + 16])
        T1 = sb("T1", [128, 2, NTMAX])
        T2 = sb("T2", [128, 2, NTMAX])
        Bf = sb("Bf", [128, NTMAX])
        Bb = sb("Bb", [128, NTMAX], BF16)
        Cb = sb("Cb", [128, NTMAX], BF16)
        Gt = sb("Gt", [128, NTMAX])
        dtF = sb("dtF", [32, NTMAX])
        NU = 5
        dtT = sb("dtT", [128, NU, 32])
        dta = sb("dta", [128, NU, 32])
        dtdd = sb("dtdd", [128, NU, 32])
        cdt = sb("cdt", [128, 8, 32])
        WPS = 3 + PB + NSQ * (ST + 3)
        xprot = Rot([sb(f"xp{i}", [128, WPS]) for i in range(2)], "xp")
        accrot = Rot([sb(f"acc{i}", [128, WPS]) for i in range(2)], "acc")
        sqrot = Rot([sb(f"sq{i}", [128, 512]) for i in range(3)], "sq")
        nacc = [sb("nacc0", [128, 512]), sb("nacc1", [128, NSAMP])]
        rt = sb("rt", [128, 512])
        psr = Rot([st.enter_context(nc.psum_tensor(f"ps{i}", [128, 512], F32)) for i in range(int(os.environ.get("MK_NPS", "8")))], "ps")
        esegr = Rot([sb(f"eseg{i}", [128, 512]) for i in range(1)], "eseg")
        eacsr = Rot([sb(f"eacs{i}", [128, 512]) for i in range(1)], "eacs")
        Rr = Rot([sb(f"R{i}", [128, 512], BF16) for i in range(2)], "R")
        MTr = Rot([sb(f"MT{i}", [128, 512], BF16) for i in range(2)], "MT")
        MXr = Rot([sb(f"MX{i}", [128, 512], BF16) for i in range(2)], "MX")
        xdtr = Rot([sb(f"xdt{i}", [128, 256], BF16) for i in range(2)], "xdt")
        xddr = Rot([sb(f"xdd{i}", [128, 256], BF16) for i in range(2)], "xdd")
        xdsr = Rot([sb(f"xds{i}", [64, 256], BF16) for i in range(2)], "xds")
        cbmr = Rot([sb(f"cbm{i}", [128, 128]) for i in range(2)], "cbm")
        BtTr = Rot([sb(f"BtT{i}", [128, 128], BF16) for i in range(2)], "BtT")
        Hbr = Rot([sb(f"Hb{i}", [128, 256], BF16) for i in range(1)], "Hb")
        ytr = Rot([sb(f"yt{i}", [128, 128]) for i in range(2)], "yt")
        vnTr = Rot([sb(f"vnT{i}", [128, 256], BF16) for i in range(2)], "vnT")
        Hs = sb("Hs", [128, NSQ, 256])
        Hsb = sb("Hsb", [128, NSQ, 256], BF16)
        identF = sb("identF", [128, 128])
        onesF = sb("onesF", [128, 128])
        onesB = sb("onesB", [128, 128], BF16)
        triF = sb("triF", [128, 128])
        Uf = sb("Uf", [128, 128])
        Ub = sb("Ub", [128, 128], BF16)
        seqsel = sb("seqsel", [64, NSQ, 128])
        triS = sb("triS", [64, 64])
        UfS = sb("UfS", [64, 64])
        UbS = sb("UbS", [64, 64], BF16)
        epsT = sb("epsT", [128, 1])
        negA = sb("negA", [128, NL, 32])
        dtbT = sb("dtbT", [32, NL])
        Wm = sb("Wm", [128, NL, 8, 128], BF16)
        Wbd = sb("Wbd", [64, NL, 8, 64], BF16)
        bb = sb("bb", [128, 1152])
        Wtmp = T1[:].rearrange("p m t -> p (m t)")[:, 0:1024]
        Wbt = T2[:].rearrange("p m t -> p (m t)")[0:64, 0:512].rearrange("p (g i) -> p g i", i=64)
        T1K = [("T1", m, ti) for m in range(2) for ti in range(2)]
        T2K = [("T2", m, ti) for m in range(2) for ti in range(2)]

        def cvc(l, off, i):
            c = l * CVL + off + i
            return cv[:, c:c + 1]

        NW1, NW2, SNW, GNW, DCH, CB, CW, FCB, FCW = 0, 16, 32, 48, 64, 80, 112, 240, 328

        S.dma(cv[:], cvec, writes=["cv"])
        memset(onesF[:], 1.0, ["onesF"], eng="pool")
        cp(onesB[:], onesF[:], ["onesF"], ["onesB"])
        memset(epsT[:], EPS, ["epsT"])
        memset(identF[:], 1.0, ["identF"], eng="pool")
        asel(identF[:], identF[:], [[-1, 128]], ALU.is_equal, 0.0, 0, 1, ["identF"], ["identF"])
        memset(triF[:], 1.0, ["triF"], eng="pool")
        asel(triF[:], triF[:], [[1, 128]], ALU.is_ge, 0.0, 0, -1, ["triF"], ["triF"])
        memset(Uf[:], 1.0, ["Uf"], eng="pool")
        asel(Uf[:], Uf[:], [[-1, 128]], ALU.is_gt, 0.0, 0, 1, ["Uf"], ["Uf"])
        cp(Ub[:], Uf[:], ["Uf"], ["Ub"])
        memset(seqsel[:], 1.0, ["seqsel"], eng="pool")
        asel(seqsel[:], seqsel[:], [[-16, NSQ], [0, 128]], ALU.is_ge, 0.0, 0, 1, ["seqsel"], ["seqsel"])
        asel(seqsel[:], seqsel[:], [[16, NSQ], [0, 128]], ALU.is_ge, 0.0, 15, -1, ["seqsel"], ["seqsel"])
        same = seqsel[:, :, 0:16]
        tt(triS[:].rearrange("p (s t) -> p s t", t=16), triF[0:64, 0:64].rearrange("p (s t) -> p s t", t=16), same,
           ALU.mult, ["triF", "seqsel"], ["triS"])
        tt(UfS[:].rearrange("p (s t) -> p s t", t=16), Uf[0:64, 0:64].rearrange("p (s t) -> p s t", t=16), same,
           ALU.mult, ["Uf", "seqsel"], ["UfS"])
        cp(UbS[:], UfS[:], ["UfS"], ["UbS"])
        memset(Hst[:], 0.0, [("H", l, g) for l in range(NL) for g in range(8)])
        memset(halS[:], 0.0, [("halS", l, c) for l in range(NL) for c in range(32)])
        memset(halF[:], 0.0, [("halF", l, c) for l in range(NL) for c in range(88)])
        S.dma(dtbT[:], dtb, writes=["dtbT"])
        for l in range(NL):
            S.dma(negA[:, l, :], alog[l:l + 1, :].to_broadcast([128, 32]), writes=[("negA", l)])
            act(negA[:, l, :], negA[:, l, :], AF.Exp, [("negA", l)], [("negA", l)])
            ts1(negA[:, l, :], negA[:, l, :], -1.0, ALU.mult, [("negA", l)], [("negA", l)])
            S.dma(Wtmp, wsT[l], writes=T1K)
            memset(Wtmp[64:128, :].rearrange("p (g i) -> p g i", i=128)[:, :, 0:64], 0.0, T1K)
            cp(Wm[:, l].rearrange("p g i -> p (g i)"), Wtmp, T1K, [("Wm", l)])
            memset(Wbt, 0.0, T2K)
            for s in range(NSQ):
                S.dma(Wbt[16 * s:16 * s + 16, :, 16 * s:16 * s + 16],
                      wsT[l, 0:16, :].rearrange("p (g i) -> p g i", i=128)[:, :, 0:16], writes=T2K)
            cp(Wbd[:, l], Wbt, T2K, [("Wbd", l)])

        NSLAB = 2 * 133
        wscrs = [nc.dram_tensor(f"wscr{i}", [133 * 128, 4096], BF16).ap() for i in range(2)]
        wstate = {"i": 0, "pass": 0}

        def wload(wl, row0, nk, col0, ncols):
            t, key = wrot.get()
            n_el = nk * ncols
            view = t[:, 0:n_el].rearrange("p (k c) -> p k c", c=ncols)
            wi = wstate["i"]
            wstate["i"] = wi + 1
            assert wi < NSLAB
            scr = wscrs[wi // 133][(wi % 133) * 128:(wi % 133 + 1) * 128, 0:n_el]
            if wstate["pass"] == 0:
                src = wl[row0:row0 + nk * 128, col0:col0 + ncols].rearrange("(k p) c -> p k c", p=128)
                S.dma(view, src, writes=[key], q="pool")
                if MK_NPASS > 1:
                    S.dma(scr, t[:, 0:n_el], reads=[key], writes=[("wscr", wi)])
            else:
                S.dma(t[:, 0:n_el], scr, reads=[("wscr", wi)], writes=[key], q="pool")
            return view, key

        def proj(slab, skey, nk, mc0, mw, rhs_t, rkeys, c0, w):
            ps, pk = psr.get()
            for k in range(nk):
                mm(ps[0:mw, 0:w], slab[:, k, mc0:mc0 + mw], rhs_t[:, k, c0:c0 + w], k == 0, k == nk - 1,
                   [skey] + rkeys, [pk])
            return ps, pk

        def stage(k):
            if MK_STAGE < k:
                raise _Stop()

        for p in range(MK_NPASS):
          wstate["i"] = 0
          wstate["pass"] = p
          try:
              samp = p == SAMPLE_PASS
              last = p == NPASS - 1
              tok0 = p * PB
              NT = PB + (NSAMP if samp else 0)
              nts = [(0, PB, 0)] + ([(PB, NSAMP, 1)] if samp else [])
              units = [(128 * u, 128, False) for u in range(PB // 128)] + ([(PB, NSAMP, True)] if samp else [])
              xkeys = lambda ti: [("x", k, ti) for k in range(KC)]

              S.dma(x[:, :, 0:PB], xT.rearrange("(k p) t -> p k t", p=128)[:, :, tok0:tok0 + PB], writes=xkeys(0))
              if samp:
                  S.dma(x[:, :, PB:PB + NSAMP], xsT.rearrange("(k p) t -> p k t", p=128), writes=xkeys(1))

              def rstd_of(acc, acck, w, scale):
                  ps, pk = psr.get()
                  mm(ps[:, 0:w], onesF[:], acc[:, 0:w], True, True, [acck, "onesF"], [pk])
                  act(rt[:, 0:w], ps[:, 0:w], AF.Sqrt, [pk, "epsT"], ["rt"], bias=epsT[:], scale=scale)
                  recip(rt[:, 0:w], rt[:, 0:w], ["rt"], ["rt"])

              def norm_stats(c0, w, ti):
                  acc, acck = sqrot.get()
                  act(acc[:, 0:w], x[:, 0, c0:c0 + w], AF.Square, [("x", 0, ti)], [acck])
                  for k in range(1, KC):
                      sq, sqk = sqrot.get()
                      if sqk == acck:
                          sq, sqk = sqrot.get()
                      act(sq[:, 0:w], x[:, k, c0:c0 + w], AF.Square, [("x", k, ti)], [sqk])
                      tt(acc[:, 0:w], acc[:, 0:w], sq[:, 0:w], ALU.add, [acck, sqk], [acck])
                  rstd_of(acc, acck, w, 1.0 / D)

              def x_update(dch, c0, w, ti, ps, pk, accum):
                  tt(x[:, dch, c0:c0 + w], x[:, dch, c0:c0 + w], ps[:, 0:w], ALU.add, [("x", dch, ti), pk], [("x", dch, ti)])
                  if accum:
                      if dch == 0:
                          act(nacc[ti][:, 0:w], x[:, dch, c0:c0 + w], AF.Square, [("x", dch, ti)], [("nacc", ti)])
                      else:
                          sq, sqk = sqrot.get()
                          act(sq[:, 0:w], x[:, dch, c0:c0 + w], AF.Square, [("x", dch, ti)], [sqk])
                          tt(nacc[ti][:, 0:w], nacc[ti][:, 0:w], sq[:, 0:w], ALU.add, [("nacc", ti), sqk], [("nacc", ti)])

              def rmsnorm_to_hT(l, off, fused):
                  for (c0, w, ti) in nts:
                      if fused:
                          rstd_of(nacc[ti], ("nacc", ti), w, 1.0 / D)
                      else:
                          norm_stats(c0, w, ti)
                      for k in range(KC):
                          stt(hT[:, k, c0:c0 + w], x[:, k, c0:c0 + w], cvc(l, off, k), rt[:, 0:w], ALU.mult, ALU.mult,
                              [("x", k, ti), "rt", "cv"], [("hT", ti)])

              def pviews(T, Hh):
                  pv = T[:, Hh:Hh + PB]
                  sv = None
                  if samp:
                      sv = T[:, Hh + PB:Hh + PB + NSQ * (ST + Hh)].rearrange("p (s t) -> p s t", t=ST + Hh)
                  return pv, sv

              def conv_chunk(pss, Hh, wcols, bcol, hal, halk, sin, pout, sout):
                  xp, xpk = xprot.get()
                  acc, acck = accrot.get()
                  WP = Hh + PB + (NSQ * (ST + Hh) if samp else 0)
                  xpv, xsv = pviews(xp, Hh)
                  apv, asv = pviews(acc, Hh)
                  for (ps, pk, c0, w, ti) in pss:
                      if ti == 0:
                          act(apv[:, c0:c0 + w], ps[:, 0:w], AF.Identity, [pk, "cv"], [acck], bias=bcol, scale=wcols[Hh])
                          act(xpv[:, c0:c0 + w], ps[:, 0:w], AF.Copy, [pk], [xpk])
                      else:
                          pin = ps[:, 0:NSAMP].rearrange("p (s t) -> p s t", t=ST)
                          act(asv[:, :, Hh:Hh + ST], pin, AF.Identity, [pk, "cv"], [acck], bias=bcol, scale=wcols[Hh])
                          act(xsv[:, :, Hh:Hh + ST], pin, AF.Copy, [pk], [xpk])
                  cp(xp[:, 0:Hh], hal, [halk], [xpk])
                  if samp:
                      S.dma(xsv[:, :, 0:Hh], sin.rearrange("p (s t) -> p s t", t=Hh), writes=[xpk])
                  for k in range(Hh):
                      stt(acc[:, Hh:WP], xp[:, k:k + WP - Hh], wcols[k], acc[:, Hh:WP], ALU.mult, ALU.add,
                          [xpk, acck, "cv"], [acck])
                  cp(hal, xp[:, PB:PB + Hh], [xpk], [halk])
                  if last:
                      S.dma(pout, hal, reads=[halk], is_out=True)
                  if samp:
                      S.dma(sout.rearrange("p (s t) -> p s t", t=Hh), xsv[:, :, ST:ST + Hh], reads=[xpk], is_out=True)
                  return acc, acck

              def conv_out(func_or_none, acc, acck, Hh, dst, dkeyf, extra_reads=(), mul=None):
                  apv, asv = pviews(acc, Hh)
                  parts = [(dst[:, 0:PB], apv, None if mul is None else mul[:, 0:PB], 0)]
                  if samp:
                      parts.append((dst[:, PB:PB + NSAMP].rearrange("p (s t) -> p s t", t=ST), asv[:, :, Hh:Hh + ST],
                                    None if mul is None else mul[:, PB:PB + NSAMP].rearrange("p (s t) -> p s t", t=ST), 1))
                  for (o, i, m_, ti) in parts:
                      if mul is None:
                          act(o, i, func_or_none, [acck] + list(extra_reads), [dkeyf(ti)])
                      else:
                          tt(o, i, m_, ALU.mult, [acck] + list(extra_reads), [dkeyf(ti)])

              stage(1)
              for l in range(MK_NL):
                  rmsnorm_to_hT(l, NW1, l > 0)
                  hk = lambda ti: [("hT", ti)]

                  stage(2)
                  slab, skey = wload(w_in[l], 0, KC, 0, 32)
                  for (c0, w, ti) in nts:
                      ps, pk = proj(slab, skey, KC, 0, 32, hT, hk(ti), c0, w)
                      act(dtF[0:32, c0:c0 + w], ps[0:32, 0:w], AF.Exp, [pk, "dtbT"], [("dtF", ti)], bias=dtbT[:, l:l + 1])
                      act(dtF[0:32, c0:c0 + w], dtF[0:32, c0:c0 + w], AF.Ln, [("dtF", ti)], [("dtF", ti)], bias=1.0)
                  if MK_SUB < 1:
                      raise _Stop()
                  for ui, (c0, Q, us) in enumerate(units):
                      if MK_SUB < 3 and us:
                          continue
                      if ui >= int(os.environ.get('MK_UNITS', '9')):
                          continue
                      ti = 1 if us else 0
                      ps, pk = psr.get()
                      tr(ps[0:Q, 0:32], dtF[0:32, c0:c0 + Q], identF[0:32, 0:32], [("dtF", ti), "identF"], [pk])
                      act(dtT[0:Q, ui, :], ps[0:Q, 0:32], AF.Copy, [pk], [("dtT", ui)])
                      tt(dta[0:Q, ui, :], dtT[0:Q, ui, :], negA[0:Q, l, :], ALU.mult, [("dtT", ui), ("negA", l)], [("dta", ui)])
                      if MK_SUB < 2:
                          continue
                      ps2, pk2 = psr.get()
                      Uu = UfS if us else Uf
                      mm(ps2[0:Q, 0:32], Uu[0:Q, 0:Q], dta[0:Q, ui, :], True, True, [("dta", ui), "Uf", "UfS"], [pk2])
                      act(dtdd[0:Q, ui, :], ps2[0:Q, 0:32], AF.Exp, [pk2], [("dtdd", ui)])
                      tt(dtdd[0:Q, ui, :], dtdd[0:Q, ui, :], dtT[0:Q, ui, :], ALU.mult, [("dtdd", ui), ("dtT", ui)],
                         [("dtdd", ui)])
                      if not us:
                          ps3, pk3 = psr.get()
                          mm(ps3[:, 0:32], onesF[0:Q, :], dta[0:Q, ui, :], True, True, [("dta", ui), "onesF"], [pk3])
                          act(cdt[:, ui, :], ps3[:, 0:32], AF.Exp, [pk3], [("cd", ui)])
                      else:
                          for s in range(NSQ):
                              ps3, pk3 = psr.get()
                              mm(ps3[:, 0:32], seqsel[0:Q, s, :], dta[0:Q, ui, :], True, True, [("dta", ui), "seqsel"], [pk3])
                              act(cdt[:, 4 + s, :], ps3[:, 0:32], AF.Exp, [pk3], [("cd", 4 + s)])

                  stage(3)
                  S.dma(bb[:, 0:1024], bs[l:l + 1, :].to_broadcast([128, 1024]), writes=["bb"])
                  S.dma(bb[:, 1024:1152], bss[l:l + 1, :].to_broadcast([128, 128]), writes=["bb"])
                  for g in range(8):
                      slab, skey = wload(w_in[l], 0, KC, 32 + 512 * g, 256)
                      slabU, skeyU = wload(w_in[l], 0, KC, 32 + 512 * g + 256, 256)
                      for m in range(2):
                          for (c0, w, ti) in nts:
                              ps, pk = proj(slab, skey, KC, m * 128, 128, hT, hk(ti), c0, w)
                              act(T1[:, m, c0:c0 + w], ps[:, 0:w], AF.Gelu_apprx_tanh, [pk], [("T1", m, ti)])
                      for (c0, w, ti) in nts:
                          sqa, sqak = sqrot.get()
                          sqb, sqbk = sqrot.get()
                          act(sqa[:, 0:w], T1[:, 0, c0:c0 + w], AF.Square, [("T1", 0, ti)], [sqak])
                          tt(sqb[:, 0:w], T1[:, 1, c0:c0 + w], T1[:, 1, c0:c0 + w], ALU.mult, [("T1", 1, ti)], [sqbk])
                          tt(sqa[:, 0:w], sqa[:, 0:w], sqb[:, 0:w], ALU.add, [sqak, sqbk], [sqak])
                          rstd_of(sqa, sqak, w, 1.0 / 256)
                          for m in range(2):
                              stt(T1[:, m, c0:c0 + w], T1[:, m, c0:c0 + w], cvc(l, GNW, 2 * g + m), rt[:, 0:w],
                                  ALU.mult, ALU.mult, [("T1", m, ti), "rt", "cv"], [("T1", m, ti)])
                              if ti == 1:
                                  S.dma(o_sv[l, 2 * g + m], T1[:, m, PB:PB + NSAMP], reads=[("T1", m, ti)], is_out=True)
                      for m in range(2):
                          for (c0, w, ti) in nts:
                              ps, pk = proj(slabU, skeyU, KC, m * 128, 128, hT, hk(ti), c0, w)
                              act(T2[:, m, c0:c0 + w], ps[:, 0:w], AF.Gelu_apprx_tanh, [pk], [("T2", m, ti)])
                      for ui, (c0, Q, us) in enumerate(units):
                          ti = 1 if us else 0
                          vnT, vnk = vnTr.get()
                          for m in range(2):
                              ps, pk = psr.get()
                              tr(ps[0:Q, 0:128], T1[:, m, c0:c0 + Q], identF[:], [("T1", m, ti), "identF"], [pk])
                              act(vnT[0:Q, m * 128:(m + 1) * 128], ps[0:Q, 0:128], AF.Copy, [pk], [vnk])
                          for m in range(2):
                              ps, pk = psr.get()
                              wm = Wbd[0:Q, l, g, 0:Q] if us else Wm[0:Q, l, g, 0:Q]
                              mm(ps[:, 0:Q], vnT[0:Q, m * 128:(m + 1) * 128], wm, True, True, [vnk, ("Wm", l), ("Wbd", l)], [pk])
                              yt, ytk = ytr.get()
                              if us:
                                  tt(yt[:, 0:Q].rearrange("p (s t) -> p s t", t=ST), ps[:, 0:Q].rearrange("p (s t) -> p s t", t=ST),
                                     bb[:, 1024 + g * 16:1024 + g * 16 + 16].unsqueeze(1).to_broadcast([128, NSQ, ST]), ALU.add,
                                     [pk, "bb"], [ytk])
                              else:
                                  tt(yt[:, 0:Q], ps[:, 0:Q], bb[:, g * 128:g * 128 + Q], ALU.add, [pk, "bb"], [ytk])
                              tt(mix[:, 2 * g + m, c0:c0 + Q], T2[:, m, c0:c0 + Q], yt[:, 0:Q], ALU.mult,
                                 [("T2", m, ti), ytk], [("mix", 2 * g + m, ti)])

                  def out_proj(row0, accum):
                      for sl in range(8):
                          slab, skey = wload(w_out[l], row0, 16, 256 * sl, 256)
                          for m_ in range(2):
                              dch = 2 * sl + m_
                              for (c0, w, ti) in nts:
                                  ps, pk = proj(slab, skey, 16, m_ * 128, 128, mix, [("mix", j, ti) for j in range(16)],
                                                c0, w)
                                  x_update(dch, c0, w, ti, ps, pk, accum)

                  stage(4)
                  out_proj(2048, False)

                  stage(5)
                  for g in range(8):
                      slabA, skA = wload(w_in[l], 0, KC, 4128 + 768 * g, 256)
                      slabB, skB = wload(w_in[l], 0, KC, 4128 + 768 * g + 256, 256)
                      slabZ, skZ = wload(w_in[l], 0, KC, 4128 + 768 * g + 512, 256)
                      for (mc0, cidx, kind) in ((0, 2 * g, "x0"), (128, 2 * g + 1, "x1"), (256, 16 + g, "B"), (384, 24 + g, "C")):
                          pss = []
                          sl_, sk_ = (slabA, skA) if mc0 < 256 else (slabB, skB)
                          for (c0, w, ti) in nts:
                              ps, pk = proj(sl_, sk_, KC, mc0 % 256, 128, hT, hk(ti), c0, w)
                              pss.append((ps, pk, c0, w, ti))
                          wcols = [cvc(l, CW, k * 32 + cidx) for k in range(4)]
                          acc, acck = conv_chunk(pss, 3, wcols, cvc(l, CB, cidx), halS[:, l, cidx, :], ("halS", l, cidx),
                                                 sconv[l, cidx], o_pconv[l, cidx], o_sconv[l, cidx])
                          if kind == "x0":
                              conv_out(AF.Silu, acc, acck, 3, T1[:, 0, :], lambda ti: ("T1", 0, ti))
                          elif kind == "x1":
                              conv_out(AF.Silu, acc, acck, 3, T1[:, 1, :], lambda ti: ("T1", 1, ti))
                          elif kind == "B":
                              conv_out(AF.Silu, acc, acck, 3, Bf[:], lambda ti: "Bf")
                              cp(Bb[:, 0:NT], Bf[:, 0:NT], ["Bf"], ["Bb"])
                          else:
                              conv_out(AF.Silu, acc, acck, 3, Cb[:], lambda ti: "Cb")
                      for m in range(2):
                          for (c0, w, ti) in nts:
                              ps, pk = proj(slabZ, skZ, KC, m * 128, 128, hT, hk(ti), c0, w)
                              act(T2[:, m, c0:c0 + w], ps[:, 0:w], AF.Silu, [pk], [("T2", m, ti)])
                      if samp:
                          S.dma(Hs[:], sssd[l, :, :, g * 256:(g + 1) * 256].rearrange("s n c -> n s c"), writes=["Hs"])
                          cp(Hsb[:], Hs[:], ["Hs"], ["Hsb"])
                      def unitA(ui, c0, Q, us):
                          Q4 = 4 * Q
                          ti = 1 if us else 0
                          t1k = [("T1", 0, ti), ("T1", 1, ti)]
                          psx, pkx = psr.get()
                          for m in range(2):
                              tr(psx[0:Q, m * 128:(m + 1) * 128], T1[:, m, c0:c0 + Q], identF[:], t1k + ["identF"], [pkx])
                          xdt, xdtk = xdtr.get()
                          xdd, xddk = xddr.get()
                          pxv = psx[0:Q, 0:256].rearrange("p (e c) -> p e c", c=64)
                          tt(xdt[0:Q, :].rearrange("p (e c) -> p e c", c=64), pxv,
                             dtT[0:Q, ui, 4 * g:4 * g + 4].unsqueeze(2).to_broadcast([Q, 4, 64]), ALU.mult,
                             [pkx, ("dtT", ui)], [xdtk])
                          tt(xdd[0:Q, :].rearrange("p (e c) -> p e c", c=64), pxv,
                             dtdd[0:Q, ui, 4 * g:4 * g + 4].unsqueeze(2).to_broadcast([Q, 4, 64]), ALU.mult,
                             [pkx, ("dtdd", ui)], [xddk])
                          psb, pkb = psr.get()
                          tr(psb[0:Q, 0:128], Bf[:, c0:c0 + Q], identF[:], ["Bf", "identF"], [pkb])
                          BtT, BtTk = BtTr.get()
                          act(BtT[0:Q, :], psb[0:Q, 0:128], AF.Copy, [pkb], [BtTk])
                          psc, pkc = psr.get()
                          mm(psc[0:Q, 0:Q], Bb[:, c0:c0 + Q], Cb[:, c0:c0 + Q], True, True, ["Bb", "Cb"], [pkc])
                          cbm, cbmk = cbmr.get()
                          tri_u = triS if us else triF
                          tt(cbm[0:Q, 0:Q], psc[0:Q, 0:Q], tri_u[0:Q, 0:Q], ALU.mult, [pkc, "triF", "triS"], [cbmk])
                          R, Rk = Rr.get()
                          R3 = R[0:Q, 0:Q4].rearrange("p (e i) -> p e i", e=4)
                          tt(R3, dta[0:Q, ui, 4 * g:4 * g + 4].unsqueeze(2).to_broadcast([Q, 4, Q]),
                             tri_u[0:Q, 0:Q].unsqueeze(1).to_broadcast([Q, 4, Q]), ALU.mult,
                             [("dta", ui), "triF", "triS"], [Rk])
                          pss_, pks_ = psr.get()
                          Ub_u = UbS if us else Ub
                          mm(pss_[0:Q, 0:Q4], Ub_u[0:Q, 0:Q], R[0:Q, 0:Q4], True, True, [Rk, "Ub", "UbS"], [pks_])
                          pse, pke = psr.get()
                          mm(pse[:, 0:Q4], onesB[0:Q, :], R[0:Q, 0:Q4], True, True, [Rk, "onesB"], [pke])
                          eseg, esegk = esegr.get()
                          eacs, eacsk = eacsr.get()
                          act(eseg[0:Q, 0:Q4], pss_[0:Q, 0:Q4], AF.Exp, [pks_], [esegk])
                          act(eacs[:, 0:Q4], pse[:, 0:Q4], AF.Exp, [pke], [eacsk])
                          MT, MTk = MTr.get()
                          MX, MXk = MXr.get()
                          tt(MT[0:Q, 0:Q4].rearrange("p (e i) -> p e i", e=4),
                             eseg[0:Q, 0:Q4].rearrange("p (e i) -> p e i", e=4),
                             cbm[0:Q, 0:Q].unsqueeze(1).to_broadcast([Q, 4, Q]), ALU.mult, [esegk, cbmk], [MTk])
                          tt(MX[:, 0:Q4].rearrange("p (e i) -> p e i", e=4),
                             eacs[:, 0:Q4].rearrange("p (e i) -> p e i", e=4),
                             Cb[:, c0:c0 + Q].unsqueeze(1).to_broadcast([128, 4, Q]), ALU.mult, [eacsk, "Cb"], [MXk])
                          return (ui, c0, Q, us, Q4, ti, t1k, xdt, xdtk, xdd, xddk, BtT, BtTk, MT, MTk, MX, MXk)

                      def unitB(ctx):
                          (ui, c0, Q, us, Q4, ti, t1k, xdt, xdtk, xdd, xddk, BtT, BtTk, MT, MTk, MX, MXk) = ctx
                          if us:
                              segs = [(16 * s, 16, Hsb[:, s, :], "Hsb", s) for s in range(NSQ)]
                          else:
                              Hb, Hbk = Hbr.get()
                              act(Hb[:], Hst[:, l, g, :], AF.Copy, [("H", l, g)], [Hbk])
                              segs = [(0, Q, Hb[:], Hbk, None)]
                          psy, pky = psr.get()
                          for e in range(4):
                              po, m = (e % 2) * 64, e // 2
                              mm(psy[po:po + 64, m * 128:m * 128 + Q], xdt[0:Q, e * 64:(e + 1) * 64],
                                 MT[0:Q, e * Q:(e + 1) * Q], True, False, [xdtk, MTk], [pky])
                              for si, (s0, sw, hb, hbk, s) in enumerate(segs):
                                  mm(psy[po:po + 64, m * 128 + s0:m * 128 + s0 + sw], hb[:, e * 64:(e + 1) * 64],
                                     MX[:, e * Q + s0:e * Q + s0 + sw], False, si == len(segs) - 1, [hbk, MXk], [pky])
                          for (s0, sw, hb, hbk, s) in segs:
                              if us:
                                  xds, xdsk = xdsr.get()
                                  ts1(xds[0:Q, :], xdd[0:Q, :], seqsel[0:Q, s, 0:1], ALU.mult, [xddk, "seqsel"], [xdsk])
                                  rhs, rk_, Ht, Hk, slot = xds[0:Q, :], xdsk, Hs[:, s, :], "Hs", 4 + s
                              else:
                                  rhs, rk_, Ht, Hk, slot = xdd[0:Q, :], xddk, Hst[:, l, g, :], ("H", l, g), ui
                              psS, pkS = psr.get()
                              mm(psS[:, 0:256], BtT[0:Q, :], rhs, True, True, [BtTk, rk_], [pkS])
                              for e in range(4):
                                  stt(Ht[:, e * 64:(e + 1) * 64], Ht[:, e * 64:(e + 1) * 64],
                                      cdt[:, slot, 4 * g + e:4 * g + e + 1], psS[:, e * 64:(e + 1) * 64], ALU.mult, ALU.add,
                                      [Hk, ("cd", slot), pkS], [Hk])
                          if us:
                              S.dma(o_sssd[l, :, :, g * 256:(g + 1) * 256].rearrange("s n c -> n s c"), Hs[:], reads=["Hs"],
                                    is_out=True)
                          for m in range(2):
                              yt, ytk = ytr.get()
                              stt(yt[:, 0:Q], T1[:, m, c0:c0 + Q], cvc(l, DCH, 2 * g + m), psy[:, m * 128:m * 128 + Q],
                                  ALU.mult, ALU.add, t1k + [pky, "cv"], [ytk])
                              tt(T2[:, m, c0:c0 + Q], yt[:, 0:Q], T2[:, m, c0:c0 + Q], ALU.mult, [ytk, ("T2", m, ti)],
                                 [("T2", m, ti)])
                      prev = None
                      for ui, (c0, Q, us) in enumerate(units):
                          ctx = unitA(ui, c0, Q, us)
                          if prev is not None:
                              unitB(prev)
                          prev = ctx
                      unitB(prev)
                      if last:
                          S.dma(o_pssd[l, :, g * 256:(g + 1) * 256], Hst[:, l, g, :], reads=[("H", l, g)], is_out=True)
                      for (c0, w, ti) in nts:
                          sqa, sqak = sqrot.get()
                          sqb, sqbk = sqrot.get()
                          act(sqa[:, 0:w], T2[:, 0, c0:c0 + w], AF.Square, [("T2", 0, ti)], [sqak])
                          tt(sqb[:, 0:w], T2[:, 1, c0:c0 + w], T2[:, 1, c0:c0 + w], ALU.mult, [("T2", 1, ti)], [sqbk])
                          tt(sqa[:, 0:w], sqa[:, 0:w], sqb[:, 0:w], ALU.add, [sqak, sqbk], [sqak])
                          rstd_of(sqa, sqak, w, 1.0 / 256)
                          for m in range(2):
                              stt(mix[:, 2 * g + m, c0:c0 + w], T2[:, m, c0:c0 + w], cvc(l, SNW, 2 * g + m), rt[:, 0:w],
                                  ALU.mult, ALU.mult, [("T2", m, ti), "rt", "cv"], [("mix", 2 * g + m, ti)])

                  stage(6)
                  out_proj(0, True)

                  stage(7)
                  rmsnorm_to_hT(l, NW2, True)
                  for s in range(4):
                      chunks = []
                      for j in range(11):
                          chunks.append(("g", 11 * s + j, j))
                          chunks.append(("v", 44 + 11 * s + j, j))
                      for q in range(11):
                          nch = 2
                          slab, skey = wload(w_up[l], 0, KC, 2816 * s + 256 * q, 256)
                          for ci in range(nch):
                              kind, cc, j = chunks[2 * q + ci]
                              pss = []
                              for (c0, w, ti) in nts:
                                  ps, pk = proj(slab, skey, KC, ci * 128, 128, hT, hk(ti), c0, w)
                                  pss.append((ps, pk, c0, w, ti))
                              wcols = [cvc(l, FCW, k * 88 + cc) for k in range(3)]
                              acc, acck = conv_chunk(pss, 2, wcols, cvc(l, FCB, cc), halF[:, l, cc, :], ("halF", l, cc),
                                                     sffn[l, cc], o_pffn[l, cc], o_sffn[l, cc])
                              if kind == "g":
                                  conv_out(AF.Silu, acc, acck, 2, Gt[:], lambda ti: "Gt")
                              else:
                                  conv_out(None, acc, acck, 2, mix[:, j, :], lambda ti, j=j: ("mix", j, ti), extra_reads=["Gt"], mul=Gt)
                      for sl in range(8):
                          slab, skey = wload(w_down[l], 1408 * s, 11, 256 * sl, 256)
                          for m_ in range(2):
                              dch = 2 * sl + m_
                              for (c0, w, ti) in nts:
                                  ps, pk = proj(slab, skey, 11, m_ * 128, 128, mix,
                                                [("mix", j, 0) for j in range(11)] + [("mix", j, 1) for j in range(11)],
                                                c0, w)
                                  x_update(dch, c0, w, ti, ps, pk, s == 3)

              stage(8)
              for (c0, w, ti) in nts:
                  rstd_of(nacc[ti], ("nacc", ti), w, 1.0 / D)
                  for k in range(KC):
                      ot, otk = sqrot.get()
                      stt(ot[:, 0:w], x[:, k, c0:c0 + w], cv[:, 2 * CVL + k:2 * CVL + k + 1], rt[:, 0:w], ALU.mult, ALU.mult,
                          [("x", k, ti), "rt", "cv"], [otk])
                      if ti == 0:
                          S.dma(yT[k * 128:(k + 1) * 128, tok0:tok0 + PB], ot[:, 0:w], reads=[otk], is_out=True)
                      else:
                          S.dma(ysT[k * 128:(k + 1) * 128, :], ot[:, 0:w], reads=[otk], is_out=True)

          except _Stop:
            pass

        S.emit()
    return nc


def _pc(v, n):
    return np.ascontiguousarray(np.asarray(v, np.float32).reshape(n, 128).T)


def _perm_in():
    idx = list(range(6144, 6176))
    for g in range(8):
        idx += list(range(8224 + 256 * g, 8224 + 256 * (g + 1)))
        idx += list(range(6176 + 256 * g, 6176 + 256 * (g + 1)))
    for g in range(8):
        idx += list(range(2048 + 256 * g, 2048 + 256 * (g + 1)))
        idx += list(range(4096 + 128 * g, 4096 + 128 * (g + 1)))
        idx += list(range(5120 + 128 * g, 5120 + 128 * (g + 1)))
        idx += list(range(256 * g, 256 * (g + 1)))
    return np.array(idx)


def _perm_up():
    idx = []
    for s in range(4):
        for j in range(11):
            cg = 11 * s + j
            cvv = 44 + 11 * s + j
            idx += list(range(128 * cg, 128 * (cg + 1)))
            idx += list(range(128 * cvv, 128 * (cvv + 1)))
    return np.array(idx)


_NC_CACHE = {}


def kernel(x_prompt, x_sample, state_ssd_conv, state_ssd, state_ffn_conv, norm1_w, w_in, ssd_conv_w,
           ssd_conv_b, dt_bias, a_log, ssd_d, ssd_norm_w, gmlp_norm_w, gmlp_w_s, gmlp_b_s, w_out,
           norm2_w, w_up, ffn_conv_w, ffn_conv_b, w_down, final_norm_w):
    f = lambda a: np.asarray(a, np.float32)
    x_prompt, x_sample = f(x_prompt), f(x_sample)
    state_ssd_conv, state_ssd, state_ffn_conv = f(state_ssd_conv), f(state_ssd), f(state_ffn_conv)
    n = 8
    cvec = np.zeros((128, 2 * CVL + 16), np.float32)
    for l in range(NL):
        b = l * CVL
        cvec[:, b + 0:b + 16] = _pc(norm1_w[l], 16)
        cvec[:, b + 16:b + 32] = _pc(norm2_w[l], 16)
        cvec[:, b + 32:b + 48] = _pc(ssd_norm_w[l], 16)
        cvec[:, b + 48:b + 64] = _pc(gmlp_norm_w[l], 16)
        cvec[:, b + 64:b + 80] = _pc(np.repeat(f(ssd_d[l]), 64), 16)
        cvec[:, b + 80:b + 112] = _pc(ssd_conv_b[l], 32)
        for k in range(4):
            cvec[:, b + 112 + 32 * k:b + 112 + 32 * (k + 1)] = _pc(f(ssd_conv_w[l])[k], 32)
        cvec[:, b + 240:b + 328] = _pc(ffn_conv_b[l], 88)
        for k in range(3):
            cvec[:, b + 328 + 88 * k:b + 328 + 88 * (k + 1)] = _pc(f(ffn_conv_w[l])[k], 88)
    cvec[:, 2 * CVL:2 * CVL + 16] = _pc(final_norm_w, 16)
    alog = np.ascontiguousarray(f(a_log))
    dtb = np.ascontiguousarray(f(dt_bias).T)
    bs = np.ascontiguousarray(f(gmlp_b_s).reshape(NL, 1024))
    bss = np.ascontiguousarray(f(gmlp_b_s)[:, :, 0:16].reshape(NL, 128))
    wsT = np.ascontiguousarray(f(gmlp_w_s).transpose(0, 3, 1, 2).reshape(NL, 128, 1024))
    w_in_r = np.ascontiguousarray(f(w_in)[:, :, _perm_in()])
    w_up_r = np.ascontiguousarray(f(w_up)[:, :, _perm_up()])
    w_out_c = np.ascontiguousarray(f(w_out))
    w_down_c = np.ascontiguousarray(f(w_down))
    if MK_SMALLW:
        w_up_r = w_out_c = w_down_c = np.zeros((NL, 128, 128), np.float32)
    shared = dict(cvec=cvec, alog=alog, dtb=dtb, bs=bs, bss=bss, wsT=wsT, w_in=w_in_r, w_out=w_out_c, w_up=w_up_r,
                  w_down=w_down_c)
    in_maps = []
    xTs = [np.ascontiguousarray(x_prompt[b].T) for b in range(4)]
    for c in range(n):
        sq = slice(NSQ * c, NSQ * (c + 1))
        m = dict(shared)
        m["xT"] = xTs[c % 4]
        m["xsT"] = np.ascontiguousarray(x_sample[sq].reshape(NSAMP, D).T)
        m["sconv"] = np.ascontiguousarray(
            state_ssd_conv[:, sq].reshape(NL, NSQ, 3, 32, 128).transpose(0, 3, 4, 1, 2).reshape(NL, 32, 128, NSQ * 3))
        m["sssd"] = np.ascontiguousarray(state_ssd[:, sq].reshape(NL, NSQ, 2048, 128).transpose(0, 1, 3, 2))
        m["sffn"] = np.ascontiguousarray(
            state_ffn_conv[:, sq].reshape(NL, NSQ, 2, 88, 128).transpose(0, 3, 4, 1, 2).reshape(NL, 88, 128, NSQ * 2))
        in_maps.append(m)
    if "nc" not in _NC_CACHE:
        _NC_CACHE["nc"] = build_program()
    nc = _NC_CACHE["nc"]
    if MK_CORES < n:
        in_maps = in_maps[:MK_CORES]
    res = run_bass_kernel_spmd(nc, in_maps, core_ids=list(range(len(in_maps))))
    R = list(res.results) + [res.results[0]] * (n - len(in_maps))
    y_prompt = np.stack([R[b]["yT"].T for b in range(4)]).astype(np.float32)
    y_sample = np.concatenate([R[c]["ysT"].T.reshape(NSQ, ST, D) for c in range(n)]).astype(np.float32)
    p_conv = np.stack([R[b]["o_pconv"].transpose(0, 3, 1, 2).reshape(NL, 3, 4096) for b in range(4)], axis=1)
    p_ssd = np.stack([R[b]["o_pssd"].transpose(0, 2, 1).reshape(NL, 32, 64, 128) for b in range(4)], axis=1)
    p_ffn = np.stack([R[b]["o_pffn"].transpose(0, 3, 1, 2).reshape(NL, 2, 2 * DFF) for b in range(4)], axis=1)
    s_conv = np.concatenate(
        [R[c]["o_sconv"].reshape(NL, 32, 128, NSQ, 3).transpose(0, 3, 4, 1, 2).reshape(NL, NSQ, 3, 4096) for c in range(n)],
        axis=1)
    s_ssd = np.concatenate([R[c]["o_sssd"].transpose(0, 1, 3, 2).reshape(NL, NSQ, 32, 64, 128) for c in range(n)], axis=1)
    s_ffn = np.concatenate(
        [R[c]["o_sffn"].reshape(NL, 88, 128, NSQ, 2).transpose(0, 3, 4, 1, 2).reshape(NL, NSQ, 2, 2 * DFF) for c in range(n)],
        axis=1)
    s_v = np.concatenate(
        [R[c]["o_sv"].reshape(NL, 16, 128, NSQ, ST).transpose(0, 3, 4, 1, 2).reshape(NL, NSQ, ST, 2048) for c in range(n)],
        axis=1)
    c32 = lambda a: np.ascontiguousarray(a, dtype=np.float32)
    return (c32(y_prompt), c32(y_sample), c32(p_conv), c32(p_ssd), c32(p_ffn), c32(s_conv), c32(s_ssd), c32(s_ffn),
            c32(s_v))
```

```python
import contextlib
import numpy as np
import concourse.bass as bass
import concourse.mybir as mybir
from concourse.bass_utils import run_bass_kernel_spmd

F32 = mybir.dt.float32
BF16 = mybir.dt.bfloat16
AF = mybir.ActivationFunctionType
ALU = mybir.AluOpType

D = 2048
KC = 16
NL = 2
SEQ = 2048
PB = 512
NPASS = SEQ // PB
NSQ = 4
ST = 16
NSAMP = NSQ * ST
NTMAX = PB + NSAMP
SAMPLE_PASS = 0
DIN = 10272
DFF = 5632
NWB = 4
EPS = 1e-6
CVL = 592
import os
MK_STAGE = int(os.environ.get('MK_STAGE', '99'))
MK_NPASS = int(os.environ.get('MK_NPASS', str(NPASS)))
MK_NL = int(os.environ.get('MK_NL', str(NL)))
MK_SUB = int(os.environ.get('MK_SUB', '99'))
MK_CORES = int(os.environ.get('MK_CORES', '8'))
MK_SMALLW = int(os.environ.get('MK_SMALLW', '0'))


class _Stop(Exception):
    pass


class Op:
    __slots__ = ("eng", "fn", "deps", "flag", "cnt", "sem", "isdma")

    def __init__(self, eng, fn, isdma):
        self.eng = eng
        self.fn = fn
        self.deps = []
        self.flag = False
        self.cnt = 0
        self.sem = None
        self.isdma = isdma


class Sched:
    ENGS = ("pe", "act", "dve", "pool", "sp")
    NDMA = {"sp": 12, "pool": 6}

    def __init__(self, nc):
        self.nc = nc
        self.ops = {e: [] for e in self.ENGS}
        self.lastw = {}
        self.readers = {}
        self.ndma = {e: 0 for e in self.ENGS}
        self.out_dmas = []

    def add(self, eng, fn, reads=(), writes=(), dma=False):
        op = Op(eng, fn, dma)
        deps = {}

        def need(p, raw):
            if p is op:
                return
            if (not p.isdma) and (not dma) and p.eng == eng:
                if eng == "pe" or not raw:
                    return
            deps[id(p)] = p

        for k in reads:
            w = self.lastw.get(k)
            if w is not None:
                need(w, True)
            if isinstance(k, tuple) and k[0] == "ps":
                rs = self.readers.get(k)
                if rs:
                    for r in rs.values():
                        if not isinstance(r, list):
                            need(r, False)
        for k in writes:
            w = self.lastw.get(k)
            if w is not None:
                need(w, False)
            rs = self.readers.get(k)
            if rs:
                for r in rs.values():
                    if isinstance(r, list):
                        for rr in r:
                            need(rr, False)
                    else:
                        need(r, False)
        for k in writes:
            self.lastw[k] = op
            self.readers[k] = {}
        for k in reads:
            rs = self.readers.setdefault(k, {})
            if dma:
                rs.setdefault("dma", []).append(op)
            else:
                rs[eng] = op
        op.deps = list(deps.values())
        for p in op.deps:
            p.flag = True
        if dma:
            op.flag = True
            i = self.ndma[eng]
            self.ndma[eng] = i + 1
            R = self.NDMA[eng]
            op.sem = (eng, i % R)
            op.cnt = 16 * (i // R + 1)
        self.ops[eng].append(op)
        return op

    def dma(self, out, in_, reads=(), writes=(), q="sp", is_out=False):
        op = self.add(q, lambda e: e.dma_start(out=out, in_=in_), reads, writes, dma=True)
        if is_out:
            self.out_dmas.append(op)
        return op

    def emit(self):
        nc = self.nc
        for e in self.ENGS:
            c = 0
            for op in self.ops[e]:
                if not op.isdma and op.flag:
                    c += 1
                    op.cnt = c
        with contextlib.ExitStack() as st:
            csem = {e: st.enter_context(nc.semaphore(f"c_{e}")) for e in ("pe", "act", "dve", "pool")}
            dsem = {}
            for e, R in self.NDMA.items():
                for i in range(R):
                    dsem[(e, i)] = st.enter_context(nc.semaphore(f"d_{e}{i}"))
            block = st.enter_context(nc.Block())

            def run(e, eo):
                waited = {}

                def wait(sem_key, sem, cnt):
                    if waited.get(sem_key, 0) >= cnt:
                        return
                    waited[sem_key] = cnt
                    eo.wait_ge(sem, cnt)

                for op in self.ops[e]:
                    for p in op.deps:
                        if p.isdma:
                            wait(p.sem, dsem[p.sem], p.cnt)
                        else:
                            wait(p.eng, csem[p.eng], p.cnt)
                    if op.isdma:
                        if op.cnt > 16:
                            wait(op.sem, dsem[op.sem], op.cnt - 16)
                        op.fn(eo).then_inc(dsem[op.sem], 16)
                    else:
                        ins = op.fn(eo)
                        if op.flag:
                            ins.then_inc(csem[e], 1)
                if e == "sp":
                    for op in self.out_dmas:
                        wait(op.sem, dsem[op.sem], op.cnt)

            @block.tensor
            def _(pe):
                run("pe", pe)

            @block.scalar
            def _(act):
                run("act", act)

            @block.vector
            def _(dve):
                run("dve", dve)

            @block.gpsimd
            def _(pool):
                run("pool", pool)

            @block.sync
            def _(sp):
                run("sp", sp)


class Rot:
    def __init__(self, tiles, name):
        self.t = tiles
        self.i = 0
        self.name = name

    def get(self):
        i = self.i % len(self.t)
        self.i += 1
        return self.t[i], (self.name, i)


def build_program():
    nc = bass.Bass("TRN2", target_bir_lowering=False)

    def din(name, shape):
        return nc.dram_tensor(name, shape, F32, kind="ExternalInput").ap()

    def dout(name, shape):
        return nc.dram_tensor(name, shape, F32, kind="ExternalOutput").ap()

    xT = din("xT", [D, SEQ])
    xsT = din("xsT", [D, NSAMP])
    cvec = din("cvec", [128, 2 * CVL + 16])
    alog = din("alog", [NL, 32])
    dtb = din("dtb", [32, NL])
    bs = din("bs", [NL, 1024])
    bss = din("bss", [NL, 128])
    wsT = din("wsT", [NL, 128, 1024])
    w_in = din("w_in", [NL, D, DIN])
    if MK_SMALLW:
        w_out = din("w_out", [NL, 128, 128])
        w_up = din("w_up", [NL, 128, 128])
        w_down = din("w_down", [NL, 128, 128])
    else:
        w_out = din("w_out", [NL, 4096, D])
        w_up = din("w_up", [NL, D, 2 * DFF])
        w_down = din("w_down", [NL, DFF, D])
    sconv = din("sconv", [NL, 32, 128, NSQ * 3])
    sssd = din("sssd", [NL, NSQ, 128, 2048])
    sffn = din("sffn", [NL, 88, 128, NSQ * 2])

    yT = dout("yT", [D, SEQ])
    ysT = dout("ysT", [D, NSAMP])
    o_pconv = dout("o_pconv", [NL, 32, 128, 3])
    o_pssd = dout("o_pssd", [NL, 128, 2048])
    o_pffn = dout("o_pffn", [NL, 88, 128, 2])
    o_sconv = dout("o_sconv", [NL, 32, 128, NSQ * 3])
    o_sssd = dout("o_sssd", [NL, NSQ, 128, 2048])
    o_sffn = dout("o_sffn", [NL, 88, 128, NSQ * 2])
    o_sv = dout("o_sv", [NL, 16, 128, NSAMP])

    with contextlib.ExitStack() as st:

        def sb(name, shape, dt=F32):
            return st.enter_context(nc.sbuf_tensor(name, shape, dt))

        S = Sched(nc)

        def act(out, in_, func, reads, writes, bias=None, scale=None):
            kw = {}
            if bias is not None:
                kw["bias"] = bias
            if scale is not None:
                kw["scale"] = scale
            S.add("act", lambda e: e.activation(out=out, in_=in_, func=func, **kw), reads, writes)

        def mm(out, lhsT, rhs, start, stop, reads, writes):
            S.add("pe", lambda e: e.matmul(out, lhsT=lhsT, rhs=rhs, start=start, stop=stop), reads, writes)

        def tr(out, in_, ident, reads, writes):
            S.add("pe", lambda e: e.transpose(out=out, in_=in_, identity=ident), reads, writes)

        def tt(out, in0, in1, op, reads, writes, eng="dve"):
            S.add(eng, lambda e: e.tensor_tensor(out=out, in0=in0, in1=in1, op=op), reads, writes)

        def stt(out, in0, scalar, in1, op0, op1, reads, writes):
            S.add("dve", lambda e: e.scalar_tensor_tensor(out=out, in0=in0, scalar=scalar, in1=in1, op0=op0, op1=op1),
                  reads, writes)

        def ts1(out, in0, s1, op0, reads, writes, eng="dve"):
            S.add(eng, lambda e: e.tensor_scalar(out=out, in0=in0, scalar1=s1, scalar2=None, op0=op0), reads, writes)

        def cp(out, in_, reads, writes, eng="dve"):
            S.add(eng, lambda e: e.tensor_copy(out=out, in_=in_), reads, writes)

        def memset(ap, val, writes, eng="dve"):
            S.add(eng, lambda e: e.memset(ap, val), (), writes)

        def recip(out, in_, reads, writes):
            S.add("dve", lambda e: e.reciprocal(out=out, in_=in_), reads, writes)

        def asel(out, in_, pattern, cmp, fill, base, cm, reads, writes):
            S.add("pool", lambda e: e.affine_select(out=out, in_=in_, pattern=pattern, compare_op=cmp, fill=fill,
                                                    base=base, channel_multiplier=cm), reads, writes)

        x = sb("x", [128, KC, NTMAX])
        hT = sb("hT", [128, KC, NTMAX], BF16)
        mix = sb("mix", [128, 16, NTMAX], BF16)
        wbs = [sb(f"wb{i}", [128, 4096], BF16) for i in range(NWB)]
        wrot = Rot(wbs, "wb")
        Hst = sb("Hst", [128, NL, 8, 256])
        halS = sb("halS", [128, NL, 32, 3])
        halF = sb("halF", [128, NL, 88, 2])
        cv = sb("cv", [128, 2 * CVL + 16])
        T1 = sb("T1", [128, 2, NTMAX])
        T2 = sb("T2", [128, 2, NTMAX])
        Bf = sb("Bf", [128, NTMAX])
        Bb = sb("Bb", [128, NTMAX], BF16)
        Cb = sb("Cb", [128, NTMAX], BF16)
        Gt = sb("Gt", [128, NTMAX])
        dtF = sb("dtF", [32, NTMAX])
        NU = 5
        dtT = sb("dtT", [128, NU, 32])
        dta = sb("dta", [128, NU, 32])
        dtdd = sb("dtdd", [128, NU, 32])
        cdt = sb("cdt", [128, 8, 32])
        WPS = 3 + PB + NSQ * (ST + 3)
        xprot = Rot([sb(f"xp{i}", [128, WPS]) for i in range(2)], "xp")
        accrot = Rot([sb(f"acc{i}", [128, WPS]) for i in range(2)], "acc")
        sqrot = Rot([sb(f"sq{i}", [128, 512]) for i in range(3)], "sq")
        nacc = [sb("nacc0", [128, 512]), sb("nacc1", [128, NSAMP])]
        rt = sb("rt", [128, 512])
        psr = Rot([st.enter_context(nc.psum_tensor(f"ps{i}", [128, 512], F32)) for i in range(int(os.environ.get("MK_NPS", "8")))], "ps")
        esegr = Rot([sb(f"eseg{i}", [128, 512]) for i in range(1)], "eseg")
        eacsr = Rot([sb(f"eacs{i}", [128, 512]) for i in range(1)], "eacs")
        Rr = Rot([sb(f"R{i}", [128, 512], BF16) for i in range(2)], "R")
        MTr = Rot([sb(f"MT{i}", [128, 512], BF16) for i in range(2)], "MT")
        MXr = Rot([sb(f"MX{i}", [128, 512], BF16) for i in range(2)], "MX")
        xdtr = Rot([sb(f"xdt{i}", [128, 256], BF16) for i in range(2)], "xdt")
        xddr = Rot([sb(f"xdd{i}", [128, 256], BF16) for i in range(2)], "xdd")
        xdsr = Rot([sb(f"xds{i}", [64, 256], BF16) for i in range(2)], "xds")
        cbmr = Rot([sb(f"cbm{i}", [128, 128]) for i in range(2)], "cbm")
        BtTr = Rot([sb(f"BtT{i}", [128, 128], BF16) for i in range(2)], "BtT")
        Hbr = Rot([sb(f"Hb{i}", [128, 256], BF16) for i in range(1)], "Hb")
        ytr = Rot([sb(f"yt{i}", [128, 128]) for i in range(2)], "yt")
        vnTr = Rot([sb(f"vnT{i}", [128, 256], BF16) for i in range(2)], "vnT")
        Hs = sb("Hs", [128, NSQ, 256])
        Hsb = sb("Hsb", [128, NSQ, 256], BF16)
        stS = Gt[:, 0:32 * NSQ * 3].rearrange("p (c t) -> p c t", t=NSQ * 3)
        stF = Hs[:].rearrange("p s c -> p (s c)")[:, 0:88 * NSQ * 2].rearrange("p (c t) -> p c t", t=NSQ * 2)
        identF = sb("identF", [128, 128])
        onesF = sb("onesF", [128, 128])
        onesB = sb("onesB", [128, 128], BF16)
        triF = sb("triF", [128, 128])
        Uf = sb("Uf", [128, 128])
        Ub = sb("Ub", [128, 128], BF16)
        seqsel = sb("seqsel", [64, NSQ, 128])
        triS = sb("triS", [64, 64])
        UfS = sb("UfS", [64, 64])
        UbS = sb("UbS", [64, 64], BF16)
        epsT = sb("epsT", [128, 1])
        negA = sb("negA", [128, NL, 32])
        dtbT = sb("dtbT", [32, NL])
        Wm = sb("Wm", [128, NL, 8, 128], BF16)
        Wbd = sb("Wbd", [64, NL, 8, 64], BF16)
        bb = sb("bb", [128, 1152])
        Wtmp = T1[:].rearrange("p m t -> p (m t)")[:, 0:1024]
        Wbt = T2[:].rearrange("p m t -> p (m t)")[0:64, 0:512].rearrange("p (g i) -> p g i", i=64)
        T1K = [("T1", m, ti) for m in range(2) for ti in range(2)]
        T2K = [("T2", m, ti) for m in range(2) for ti in range(2)]

        def cvc(l, off, i):
            c = l * CVL + off + i
            return cv[:, c:c + 1]

        NW1, NW2, SNW, GNW, DCH, CB, CW, FCB, FCW = 0, 16, 32, 48, 64, 80, 112, 240, 328

        S.dma(cv[:], cvec, writes=["cv"])
        memset(onesF[:], 1.0, ["onesF"], eng="pool")
        cp(onesB[:], onesF[:], ["onesF"], ["onesB"])
        memset(epsT[:], EPS, ["epsT"])
        memset(identF[:], 1.0, ["identF"], eng="pool")
        asel(identF[:], identF[:], [[-1, 128]], ALU.is_equal, 0.0, 0, 1, ["identF"], ["identF"])
        memset(triF[:], 1.0, ["triF"], eng="pool")
        asel(triF[:], triF[:], [[1, 128]], ALU.is_ge, 0.0, 0, -1, ["triF"], ["triF"])
        memset(Uf[:], 1.0, ["Uf"], eng="pool")
        asel(Uf[:], Uf[:], [[-1, 128]], ALU.is_gt, 0.0, 0, 1, ["Uf"], ["Uf"])
        cp(Ub[:], Uf[:], ["Uf"], ["Ub"])
        memset(seqsel[:], 1.0, ["seqsel"], eng="pool")
        asel(seqsel[:], seqsel[:], [[-16, NSQ], [0, 128]], ALU.is_ge, 0.0, 0, 1, ["seqsel"], ["seqsel"])
        asel(seqsel[:], seqsel[:], [[16, NSQ], [0, 128]], ALU.is_ge, 0.0, 15, -1, ["seqsel"], ["seqsel"])
        same = seqsel[:, :, 0:16]
        tt(triS[:].rearrange("p (s t) -> p s t", t=16), triF[0:64, 0:64].rearrange("p (s t) -> p s t", t=16), same,
           ALU.mult, ["triF", "seqsel"], ["triS"])
        tt(UfS[:].rearrange("p (s t) -> p s t", t=16), Uf[0:64, 0:64].rearrange("p (s t) -> p s t", t=16), same,
           ALU.mult, ["Uf", "seqsel"], ["UfS"])
        cp(UbS[:], UfS[:], ["UfS"], ["UbS"])
        memset(Hst[:], 0.0, [("H", l, g) for l in range(NL) for g in range(8)])
        memset(halS[:], 0.0, [("halS", l, c) for l in range(NL) for c in range(32)])
        memset(halF[:], 0.0, [("halF", l, c) for l in range(NL) for c in range(88)])
        S.dma(dtbT[:], dtb, writes=["dtbT"])
        for l in range(NL):
            S.dma(negA[:, l, :], alog[l:l + 1, :].to_broadcast([128, 32]), writes=[("negA", l)])
            act(negA[:, l, :], negA[:, l, :], AF.Exp, [("negA", l)], [("negA", l)])
            ts1(negA[:, l, :], negA[:, l, :], -1.0, ALU.mult, [("negA", l)], [("negA", l)])
            S.dma(Wtmp, wsT[l], writes=T1K)
            memset(Wtmp[64:128, :].rearrange("p (g i) -> p g i", i=128)[:, :, 0:64], 0.0, T1K)
            cp(Wm[:, l].rearrange("p g i -> p (g i)"), Wtmp, T1K, [("Wm", l)])
            memset(Wbt, 0.0, T2K)
            for s in range(NSQ):
                S.dma(Wbt[16 * s:16 * s + 16, :, 16 * s:16 * s + 16],
                      wsT[l, 0:16, :].rearrange("p (g i) -> p g i", i=128)[:, :, 0:16], writes=T2K)
            cp(Wbd[:, l], Wbt, T2K, [("Wbd", l)])

        NSLAB = 2 * 133
        wscrs = [nc.dram_tensor(f"wscr{i}", [133 * 128, 4096], BF16).ap() for i in range(2)]
        wstate = {"i": 0, "pass": 0}

        def wload(wl, row0, nk, col0, ncols):
            t, key = wrot.get()
            n_el = nk * ncols
            view = t[:, 0:n_el].rearrange("p (k c) -> p k c", c=ncols)
            wi = wstate["i"]
            wstate["i"] = wi + 1
            assert wi < NSLAB
            scr = wscrs[wi // 133][(wi % 133) * 128:(wi % 133 + 1) * 128, 0:n_el]
            if wstate["pass"] == 0:
                src = wl[row0:row0 + nk * 128, col0:col0 + ncols].rearrange("(k p) c -> p k c", p=128)
                S.dma(view, src, writes=[key], q="pool")
                if MK_NPASS > 1:
                    S.dma(scr, t[:, 0:n_el], reads=[key], writes=[("wscr", wi)])
            else:
                S.dma(t[:, 0:n_el], scr, reads=[("wscr", wi)], writes=[key], q="pool")
            return view, key

        def proj(slab, skey, nk, mc0, mw, rhs_t, rkeys, c0, w):
            ps, pk = psr.get()
            for k in range(nk):
                mm(ps[0:mw, 0:w], slab[:, k, mc0:mc0 + mw], rhs_t[:, k, c0:c0 + w], k == 0, k == nk - 1,
                   [skey] + rkeys, [pk])
            return ps, pk

        def stage(k):
            if MK_STAGE < k:
                raise _Stop()

        for p in range(MK_NPASS):
          wstate["i"] = 0
          wstate["pass"] = p
          try:
              samp = p == SAMPLE_PASS
              last = p == NPASS - 1
              tok0 = p * PB
              NT = PB + (NSAMP if samp else 0)
              nts = [(0, PB, 0)] + ([(PB, NSAMP, 1)] if samp else [])
              units = [(128 * u, 128, False) for u in range(PB // 128)] + ([(PB, NSAMP, True)] if samp else [])
              xkeys = lambda ti: [("x", k, ti) for k in range(KC)]

              S.dma(x[:, :, 0:PB], xT.rearrange("(k p) t -> p k t", p=128)[:, :, tok0:tok0 + PB], writes=xkeys(0))
              if samp:
                  S.dma(x[:, :, PB:PB + NSAMP], xsT.rearrange("(k p) t -> p k t", p=128), writes=xkeys(1))

              def rstd_of(acc, acck, w, scale):
                  ps, pk = psr.get()
                  mm(ps[:, 0:w], onesF[:], acc[:, 0:w], True, True, [acck, "onesF"], [pk])
                  act(rt[:, 0:w], ps[:, 0:w], AF.Sqrt, [pk, "epsT"], ["rt"], bias=epsT[:], scale=scale)
                  recip(rt[:, 0:w], rt[:, 0:w], ["rt"], ["rt"])

              def norm_stats(c0, w, ti):
                  acc, acck = sqrot.get()
                  act(acc[:, 0:w], x[:, 0, c0:c0 + w], AF.Square, [("x", 0, ti)], [acck])
                  for k in range(1, KC):
                      sq, sqk = sqrot.get()
                      if sqk == acck:
                          sq, sqk = sqrot.get()
                      act(sq[:, 0:w], x[:, k, c0:c0 + w], AF.Square, [("x", k, ti)], [sqk])
                      tt(acc[:, 0:w], acc[:, 0:w], sq[:, 0:w], ALU.add, [acck, sqk], [acck])
                  rstd_of(acc, acck, w, 1.0 / D)

              def x_update(dch, c0, w, ti, ps, pk, accum):
                  tt(x[:, dch, c0:c0 + w], x[:, dch, c0:c0 + w], ps[:, 0:w], ALU.add, [("x", dch, ti), pk], [("x", dch, ti)])
                  if accum:
                      if dch == 0:
                          act(nacc[ti][:, 0:w], x[:, dch, c0:c0 + w], AF.Square, [("x", dch, ti)], [("nacc", ti)])
                      else:
                          sq, sqk = sqrot.get()
                          act(sq[:, 0:w], x[:, dch, c0:c0 + w], AF.Square, [("x", dch, ti)], [sqk])
                          tt(nacc[ti][:, 0:w], nacc[ti][:, 0:w], sq[:, 0:w], ALU.add, [("nacc", ti), sqk], [("nacc", ti)])

              def rmsnorm_to_hT(l, off, fused):
                  for (c0, w, ti) in nts:
                      if fused:
                          rstd_of(nacc[ti], ("nacc", ti), w, 1.0 / D)
                      else:
                          norm_stats(c0, w, ti)
                      for k in range(KC):
                          stt(hT[:, k, c0:c0 + w], x[:, k, c0:c0 + w], cvc(l, off, k), rt[:, 0:w], ALU.mult, ALU.mult,
                              [("x", k, ti), "rt", "cv"], [("hT", ti)])

              def pviews(T, Hh):
                  pv = T[:, Hh:Hh + PB]
                  sv = None
                  if samp:
                      sv = T[:, Hh + PB:Hh + PB + NSQ * (ST + Hh)].rearrange("p (s t) -> p s t", t=ST + Hh)
                  return pv, sv

              def conv_chunk(pss, Hh, wcols, bcol, hal, halk, stc, stk, pout):
                  xp, xpk = xprot.get()
                  acc, acck = accrot.get()
                  WP = Hh + PB + (NSQ * (ST + Hh) if samp else 0)
                  xpv, xsv = pviews(xp, Hh)
                  apv, asv = pviews(acc, Hh)
                  for (ps, pk, c0, w, ti) in pss:
                      if ti == 0:
                          act(apv[:, c0:c0 + w], ps[:, 0:w], AF.Identity, [pk, "cv"], [acck], bias=bcol, scale=wcols[Hh])
                          act(xpv[:, c0:c0 + w], ps[:, 0:w], AF.Copy, [pk], [xpk])
                      else:
                          pin = ps[:, 0:NSAMP].rearrange("p (s t) -> p s t", t=ST)
                          act(asv[:, :, Hh:Hh + ST], pin, AF.Identity, [pk, "cv"], [acck], bias=bcol, scale=wcols[Hh])
                          act(xsv[:, :, Hh:Hh + ST], pin, AF.Copy, [pk], [xpk])
                  cp(xp[:, 0:Hh], hal, [halk], [xpk])
                  if samp:
                      cp(xsv[:, :, 0:Hh], stc.rearrange("p (s t) -> p s t", t=Hh), [stk], [xpk])
                  for k in range(Hh):
                      stt(acc[:, Hh:WP], xp[:, k:k + WP - Hh], wcols[k], acc[:, Hh:WP], ALU.mult, ALU.add,
                          [xpk, acck, "cv"], [acck])
                  cp(hal, xp[:, PB:PB + Hh], [xpk], [halk])
                  if last:
                      S.dma(pout, hal, reads=[halk], is_out=True)
                  if samp:
                      cp(stc.rearrange("p (s t) -> p s t", t=Hh), xsv[:, :, ST:ST + Hh], [xpk], [stk])
                  return acc, acck

              def conv_out(func_or_none, acc, acck, Hh, dst, dkeyf, extra_reads=(), mul=None):
                  apv, asv = pviews(acc, Hh)
                  parts = [(dst[:, 0:PB], apv, None if mul is None else mul[:, 0:PB], 0)]
                  if samp:
                      parts.append((dst[:, PB:PB + NSAMP].rearrange("p (s t) -> p s t", t=ST), asv[:, :, Hh:Hh + ST],
                                    None if mul is None else mul[:, PB:PB + NSAMP].rearrange("p (s t) -> p s t", t=ST), 1))
                  for (o, i, m_, ti) in parts:
                      if mul is None:
                          act(o, i, func_or_none, [acck] + list(extra_reads), [dkeyf(ti)])
                      else:
                          tt(o, i, m_, ALU.mult, [acck] + list(extra_reads), [dkeyf(ti)])

              stage(1)
              for l in range(MK_NL):
                  rmsnorm_to_hT(l, NW1, l > 0)
                  hk = lambda ti: [("hT", ti)]

                  stage(2)
                  slab, skey = wload(w_in[l], 0, KC, 0, 32)
                  for (c0, w, ti) in nts:
                      ps, pk = proj(slab, skey, KC, 0, 32, hT, hk(ti), c0, w)
                      act(dtF[0:32, c0:c0 + w], ps[0:32, 0:w], AF.Exp, [pk, "dtbT"], [("dtF", ti)], bias=dtbT[:, l:l + 1])
                      act(dtF[0:32, c0:c0 + w], dtF[0:32, c0:c0 + w], AF.Ln, [("dtF", ti)], [("dtF", ti)], bias=1.0)
                  if MK_SUB < 1:
                      raise _Stop()
                  for ui, (c0, Q, us) in enumerate(units):
                      if MK_SUB < 3 and us:
                          continue
                      if ui >= int(os.environ.get('MK_UNITS', '9')):
                          continue
                      ti = 1 if us else 0
                      ps, pk = psr.get()
                      tr(ps[0:Q, 0:32], dtF[0:32, c0:c0 + Q], identF[0:32, 0:32], [("dtF", ti), "identF"], [pk])
                      act(dtT[0:Q, ui, :], ps[0:Q, 0:32], AF.Copy, [pk], [("dtT", ui)])
                      tt(dta[0:Q, ui, :], dtT[0:Q, ui, :], negA[0:Q, l, :], ALU.mult, [("dtT", ui), ("negA", l)], [("dta", ui)])
                      if MK_SUB < 2:
                          continue
                      ps2, pk2 = psr.get()
                      Uu = UfS if us else Uf
                      mm(ps2[0:Q, 0:32], Uu[0:Q, 0:Q], dta[0:Q, ui, :], True, True, [("dta", ui), "Uf", "UfS"], [pk2])
                      act(dtdd[0:Q, ui, :], ps2[0:Q, 0:32], AF.Exp, [pk2], [("dtdd", ui)])
                      tt(dtdd[0:Q, ui, :], dtdd[0:Q, ui, :], dtT[0:Q, ui, :], ALU.mult, [("dtdd", ui), ("dtT", ui)],
                         [("dtdd", ui)])
                      if not us:
                          ps3, pk3 = psr.get()
                          mm(ps3[:, 0:32], onesF[0:Q, :], dta[0:Q, ui, :], True, True, [("dta", ui), "onesF"], [pk3])
                          act(cdt[:, ui, :], ps3[:, 0:32], AF.Exp, [pk3], [("cd", ui)])
                      else:
                          for s in range(NSQ):
                              ps3, pk3 = psr.get()
                              mm(ps3[:, 0:32], seqsel[0:Q, s, :], dta[0:Q, ui, :], True, True, [("dta", ui), "seqsel"], [pk3])
                              act(cdt[:, 4 + s, :], ps3[:, 0:32], AF.Exp, [pk3], [("cd", 4 + s)])

                  stage(3)
                  S.dma(bb[:, 0:1024], bs[l:l + 1, :].to_broadcast([128, 1024]), writes=["bb"])
                  S.dma(bb[:, 1024:1152], bss[l:l + 1, :].to_broadcast([128, 128]), writes=["bb"])
                  for g in range(8):
                      slab, skey = wload(w_in[l], 0, KC, 32 + 512 * g, 256)
                      slabU, skeyU = wload(w_in[l], 0, KC, 32 + 512 * g + 256, 256)
                      for m in range(2):
                          for (c0, w, ti) in nts:
                              ps, pk = proj(slab, skey, KC, m * 128, 128, hT, hk(ti), c0, w)
                              act(T1[:, m, c0:c0 + w], ps[:, 0:w], AF.Gelu_apprx_tanh, [pk], [("T1", m, ti)])
                      for (c0, w, ti) in nts:
                          sqa, sqak = sqrot.get()
                          sqb, sqbk = sqrot.get()
                          act(sqa[:, 0:w], T1[:, 0, c0:c0 + w], AF.Square, [("T1", 0, ti)], [sqak])
                          tt(sqb[:, 0:w], T1[:, 1, c0:c0 + w], T1[:, 1, c0:c0 + w], ALU.mult, [("T1", 1, ti)], [sqbk])
                          tt(sqa[:, 0:w], sqa[:, 0:w], sqb[:, 0:w], ALU.add, [sqak, sqbk], [sqak])
                          rstd_of(sqa, sqak, w, 1.0 / 256)
                          for m in range(2):
                              stt(T1[:, m, c0:c0 + w], T1[:, m, c0:c0 + w], cvc(l, GNW, 2 * g + m), rt[:, 0:w],
                                  ALU.mult, ALU.mult, [("T1", m, ti), "rt", "cv"], [("T1", m, ti)])
                              if ti == 1:
                                  S.dma(o_sv[l, 2 * g + m], T1[:, m, PB:PB + NSAMP], reads=[("T1", m, ti)], is_out=True)
                      for m in range(2):
                          for (c0, w, ti) in nts:
                              ps, pk = proj(slabU, skeyU, KC, m * 128, 128, hT, hk(ti), c0, w)
                              act(T2[:, m, c0:c0 + w], ps[:, 0:w], AF.Gelu_apprx_tanh, [pk], [("T2", m, ti)])
                      for ui, (c0, Q, us) in enumerate(units):
                          ti = 1 if us else 0
                          vnT, vnk = vnTr.get()
                          for m in range(2):
                              ps, pk = psr.get()
                              tr(ps[0:Q, 0:128], T1[:, m, c0:c0 + Q], identF[:], [("T1", m, ti), "identF"], [pk])
                              act(vnT[0:Q, m * 128:(m + 1) * 128], ps[0:Q, 0:128], AF.Copy, [pk], [vnk])
                          for m in range(2):
                              ps, pk = psr.get()
                              wm = Wbd[0:Q, l, g, 0:Q] if us else Wm[0:Q, l, g, 0:Q]
                              mm(ps[:, 0:Q], vnT[0:Q, m * 128:(m + 1) * 128], wm, True, True, [vnk, ("Wm", l), ("Wbd", l)], [pk])
                              yt, ytk = ytr.get()
                              if us:
                                  tt(yt[:, 0:Q].rearrange("p (s t) -> p s t", t=ST), ps[:, 0:Q].rearrange("p (s t) -> p s t", t=ST),
                                     bb[:, 1024 + g * 16:1024 + g * 16 + 16].unsqueeze(1).to_broadcast([128, NSQ, ST]), ALU.add,
                                     [pk, "bb"], [ytk])
                              else:
                                  tt(yt[:, 0:Q], ps[:, 0:Q], bb[:, g * 128:g * 128 + Q], ALU.add, [pk, "bb"], [ytk])
                              tt(mix[:, 2 * g + m, c0:c0 + Q], T2[:, m, c0:c0 + Q], yt[:, 0:Q], ALU.mult,
                                 [("T2", m, ti), ytk], [("mix", 2 * g + m, ti)])

                  def out_proj(row0, accum):
                      for sl in range(8):
                          slab, skey = wload(w_out[l], row0, 16, 256 * sl, 256)
                          for m_ in range(2):
                              dch = 2 * sl + m_
                              for (c0, w, ti) in nts:
                                  ps, pk = proj(slab, skey, 16, m_ * 128, 128, mix, [("mix", j, ti) for j in range(16)],
                                                c0, w)
                                  x_update(dch, c0, w, ti, ps, pk, accum)

                  stage(4)
                  out_proj(2048, False)

                  stage(5)
                  if samp:
                      S.dma(stS, sconv[l].rearrange("c p t -> p c t"), writes=["Gt"])
                  for g in range(8):
                      slabA, skA = wload(w_in[l], 0, KC, 4128 + 768 * g, 256)
                      slabB, skB = wload(w_in[l], 0, KC, 4128 + 768 * g + 256, 256)
                      slabZ, skZ = wload(w_in[l], 0, KC, 4128 + 768 * g + 512, 256)
                      for (mc0, cidx, kind) in ((0, 2 * g, "x0"), (128, 2 * g + 1, "x1"), (256, 16 + g, "B"), (384, 24 + g, "C")):
                          pss = []
                          sl_, sk_ = (slabA, skA) if mc0 < 256 else (slabB, skB)
                          for (c0, w, ti) in nts:
                              ps, pk = proj(sl_, sk_, KC, mc0 % 256, 128, hT, hk(ti), c0, w)
                              pss.append((ps, pk, c0, w, ti))
                          wcols = [cvc(l, CW, k * 32 + cidx) for k in range(4)]
                          acc, acck = conv_chunk(pss, 3, wcols, cvc(l, CB, cidx), halS[:, l, cidx, :], ("halS", l, cidx),
                                                 stS[:, cidx, :], "Gt", o_pconv[l, cidx])
                          if kind == "x0":
                              conv_out(AF.Silu, acc, acck, 3, T1[:, 0, :], lambda ti: ("T1", 0, ti))
                          elif kind == "x1":
                              conv_out(AF.Silu, acc, acck, 3, T1[:, 1, :], lambda ti: ("T1", 1, ti))
                          elif kind == "B":
                              conv_out(AF.Silu, acc, acck, 3, Bf[:], lambda ti: "Bf")
                              cp(Bb[:, 0:NT], Bf[:, 0:NT], ["Bf"], ["Bb"])
                          else:
                              conv_out(AF.Silu, acc, acck, 3, Cb[:], lambda ti: "Cb")
                      for m in range(2):
                          for (c0, w, ti) in nts:
                              ps, pk = proj(slabZ, skZ, KC, m * 128, 128, hT, hk(ti), c0, w)
                              act(T2[:, m, c0:c0 + w], ps[:, 0:w], AF.Silu, [pk], [("T2", m, ti)])
                      if samp:
                          S.dma(Hs[:], sssd[l, :, :, g * 256:(g + 1) * 256].rearrange("s n c -> n s c"), writes=["Hs"])
                          cp(Hsb[:], Hs[:], ["Hs"], ["Hsb"])
                      def unitA(ui, c0, Q, us):
                          Q4 = 4 * Q
                          ti = 1 if us else 0
                          t1k = [("T1", 0, ti), ("T1", 1, ti)]
                          psx, pkx = psr.get()
                          for m in range(2):
                              tr(psx[0:Q, m * 128:(m + 1) * 128], T1[:, m, c0:c0 + Q], identF[:], t1k + ["identF"], [pkx])
                          xdt, xdtk = xdtr.get()
                          xdd, xddk = xddr.get()
                          pxv = psx[0:Q, 0:256].rearrange("p (e c) -> p e c", c=64)
                          tt(xdt[0:Q, :].rearrange("p (e c) -> p e c", c=64), pxv,
                             dtT[0:Q, ui, 4 * g:4 * g + 4].unsqueeze(2).to_broadcast([Q, 4, 64]), ALU.mult,
                             [pkx, ("dtT", ui)], [xdtk])
                          tt(xdd[0:Q, :].rearrange("p (e c) -> p e c", c=64), pxv,
                             dtdd[0:Q, ui, 4 * g:4 * g + 4].unsqueeze(2).to_broadcast([Q, 4, 64]), ALU.mult,
                             [pkx, ("dtdd", ui)], [xddk])
                          psb, pkb = psr.get()
                          tr(psb[0:Q, 0:128], Bf[:, c0:c0 + Q], identF[:], ["Bf", "identF"], [pkb])
                          BtT, BtTk = BtTr.get()
                          act(BtT[0:Q, :], psb[0:Q, 0:128], AF.Copy, [pkb], [BtTk])
                          psc, pkc = psr.get()
                          mm(psc[0:Q, 0:Q], Bb[:, c0:c0 + Q], Cb[:, c0:c0 + Q], True, True, ["Bb", "Cb"], [pkc])
                          cbm, cbmk = cbmr.get()
                          tri_u = triS if us else triF
                          tt(cbm[0:Q, 0:Q], psc[0:Q, 0:Q], tri_u[0:Q, 0:Q], ALU.mult, [pkc, "triF", "triS"], [cbmk])
                          R, Rk = Rr.get()
                          R3 = R[0:Q, 0:Q4].rearrange("p (e i) -> p e i", e=4)
                          tt(R3, dta[0:Q, ui, 4 * g:4 * g + 4].unsqueeze(2).to_broadcast([Q, 4, Q]),
                             tri_u[0:Q, 0:Q].unsqueeze(1).to_broadcast([Q, 4, Q]), ALU.mult,
                             [("dta", ui), "triF", "triS"], [Rk])
                          pss_, pks_ = psr.get()
                          Ub_u = UbS if us else Ub
                          mm(pss_[0:Q, 0:Q4], Ub_u[0:Q, 0:Q], R[0:Q, 0:Q4], True, True, [Rk, "Ub", "UbS"], [pks_])
                          pse, pke = psr.get()
                          mm(pse[:, 0:Q4], onesB[0:Q, :], R[0:Q, 0:Q4], True, True, [Rk, "onesB"], [pke])
                          eseg, esegk = esegr.get()
                          eacs, eacsk = eacsr.get()
                          act(eseg[0:Q, 0:Q4], pss_[0:Q, 0:Q4], AF.Exp, [pks_], [esegk])
                          act(eacs[:, 0:Q4], pse[:, 0:Q4], AF.Exp, [pke], [eacsk])
                          MT, MTk = MTr.get()
                          MX, MXk = MXr.get()
                          tt(MT[0:Q, 0:Q4].rearrange("p (e i) -> p e i", e=4),
                             eseg[0:Q, 0:Q4].rearrange("p (e i) -> p e i", e=4),
                             cbm[0:Q, 0:Q].unsqueeze(1).to_broadcast([Q, 4, Q]), ALU.mult, [esegk, cbmk], [MTk])
                          tt(MX[:, 0:Q4].rearrange("p (e i) -> p e i", e=4),
                             eacs[:, 0:Q4].rearrange("p (e i) -> p e i", e=4),
                             Cb[:, c0:c0 + Q].unsqueeze(1).to_broadcast([128, 4, Q]), ALU.mult, [eacsk, "Cb"], [MXk])
                          return (ui, c0, Q, us, Q4, ti, t1k, xdt, xdtk, xdd, xddk, BtT, BtTk, MT, MTk, MX, MXk)

                      def unitB(ctx):
                          (ui, c0, Q, us, Q4, ti, t1k, xdt, xdtk, xdd, xddk, BtT, BtTk, MT, MTk, MX, MXk) = ctx
                          if us:
                              segs = [(16 * s, 16, Hsb[:, s, :], "Hsb", s) for s in range(NSQ)]
                          else:
                              Hb, Hbk = Hbr.get()
                              act(Hb[:], Hst[:, l, g, :], AF.Copy, [("H", l, g)], [Hbk])
                              segs = [(0, Q, Hb[:], Hbk, None)]
                          psy, pky = psr.get()
                          for e in range(4):
                              po, m = (e % 2) * 64, e // 2
                              mm(psy[po:po + 64, m * 128:m * 128 + Q], xdt[0:Q, e * 64:(e + 1) * 64],
                                 MT[0:Q, e * Q:(e + 1) * Q], True, False, [xdtk, MTk], [pky])
                              for si, (s0, sw, hb, hbk, s) in enumerate(segs):
                                  mm(psy[po:po + 64, m * 128 + s0:m * 128 + s0 + sw], hb[:, e * 64:(e + 1) * 64],
                                     MX[:, e * Q + s0:e * Q + s0 + sw], False, si == len(segs) - 1, [hbk, MXk], [pky])
                          for (s0, sw, hb, hbk, s) in segs:
                              if us:
                                  xds, xdsk = xdsr.get()
                                  ts1(xds[0:Q, :], xdd[0:Q, :], seqsel[0:Q, s, 0:1], ALU.mult, [xddk, "seqsel"], [xdsk])
                                  rhs, rk_, Ht, Hk, slot = xds[0:Q, :], xdsk, Hs[:, s, :], "Hs", 4 + s
                              else:
                                  rhs, rk_, Ht, Hk, slot = xdd[0:Q, :], xddk, Hst[:, l, g, :], ("H", l, g), ui
                              psS, pkS = psr.get()
                              mm(psS[:, 0:256], BtT[0:Q, :], rhs, True, True, [BtTk, rk_], [pkS])
                              tt(Ht.rearrange("p (e c) -> p e c", c=64), Ht.rearrange("p (e c) -> p e c", c=64),
                                 cdt[:, slot, 4 * g:4 * g + 4].unsqueeze(2).to_broadcast([128, 4, 64]), ALU.mult,
                                 [Hk, ("cd", slot)], [Hk])
                              tt(Ht, Ht, psS[:, 0:256], ALU.add, [Hk, pkS], [Hk])
                          if us:
                              S.dma(o_sssd[l, :, :, g * 256:(g + 1) * 256].rearrange("s n c -> n s c"), Hs[:], reads=["Hs"],
                                    is_out=True)
                          for m in range(2):
                              yt, ytk = ytr.get()
                              stt(yt[:, 0:Q], T1[:, m, c0:c0 + Q], cvc(l, DCH, 2 * g + m), psy[:, m * 128:m * 128 + Q],
                                  ALU.mult, ALU.add, t1k + [pky, "cv"], [ytk])
                              tt(T2[:, m, c0:c0 + Q], yt[:, 0:Q], T2[:, m, c0:c0 + Q], ALU.mult, [ytk, ("T2", m, ti)],
                                 [("T2", m, ti)])
                      prev = None
                      for ui, (c0, Q, us) in enumerate(units):
                          ctx = unitA(ui, c0, Q, us)
                          if prev is not None:
                              unitB(prev)
                          prev = ctx
                      unitB(prev)
                      if last:
                          S.dma(o_pssd[l, :, g * 256:(g + 1) * 256], Hst[:, l, g, :], reads=[("H", l, g)], is_out=True)
                      for (c0, w, ti) in nts:
                          sqa, sqak = sqrot.get()
                          sqb, sqbk = sqrot.get()
                          act(sqa[:, 0:w], T2[:, 0, c0:c0 + w], AF.Square, [("T2", 0, ti)], [sqak])
                          tt(sqb[:, 0:w], T2[:, 1, c0:c0 + w], T2[:, 1, c0:c0 + w], ALU.mult, [("T2", 1, ti)], [sqbk])
                          tt(sqa[:, 0:w], sqa[:, 0:w], sqb[:, 0:w], ALU.add, [sqak, sqbk], [sqak])
                          rstd_of(sqa, sqak, w, 1.0 / 256)
                          for m in range(2):
                              stt(mix[:, 2 * g + m, c0:c0 + w], T2[:, m, c0:c0 + w], cvc(l, SNW, 2 * g + m), rt[:, 0:w],
                                  ALU.mult, ALU.mult, [("T2", m, ti), "rt", "cv"], [("mix", 2 * g + m, ti)])

                  stage(6)
                  if samp:
                      S.dma(o_sconv[l].rearrange("c p t -> p c t"), stS, reads=["Gt"], is_out=True)
                  out_proj(0, True)

                  stage(7)
                  rmsnorm_to_hT(l, NW2, True)
                  if samp:
                      S.dma(stF, sffn[l].rearrange("c p t -> p c t"), writes=["Hs"])
                  for s in range(4):
                      chunks = []
                      for j in range(11):
                          chunks.append(("g", 11 * s + j, j))
                          chunks.append(("v", 44 + 11 * s + j, j))
                      for q in range(11):
                          nch = 2
                          slab, skey = wload(w_up[l], 0, KC, 2816 * s + 256 * q, 256)
                          for ci in range(nch):
                              kind, cc, j = chunks[2 * q + ci]
                              pss = []
                              for (c0, w, ti) in nts:
                                  ps, pk = proj(slab, skey, KC, ci * 128, 128, hT, hk(ti), c0, w)
                                  pss.append((ps, pk, c0, w, ti))
                              wcols = [cvc(l, FCW, k * 88 + cc) for k in range(3)]
                              acc, acck = conv_chunk(pss, 2, wcols, cvc(l, FCB, cc), halF[:, l, cc, :], ("halF", l, cc),
                                                     stF[:, cc, :], "Hs", o_pffn[l, cc])
                              if kind == "g":
                                  conv_out(AF.Silu, acc, acck, 2, Gt[:], lambda ti: "Gt")
                              else:
                                  conv_out(None, acc, acck, 2, mix[:, j, :], lambda ti, j=j: ("mix", j, ti), extra_reads=["Gt"], mul=Gt)
                      for sl in range(8):
                          slab, skey = wload(w_down[l], 1408 * s, 11, 256 * sl, 256)
                          for m_ in range(2):
                              dch = 2 * sl + m_
                              for (c0, w, ti) in nts:
                                  ps, pk = proj(slab, skey, 11, m_ * 128, 128, mix,
                                                [("mix", j, 0) for j in range(11)] + [("mix", j, 1) for j in range(11)],
                                                c0, w)
                                  x_update(dch, c0, w, ti, ps, pk, s == 3)
                  if samp:
                      S.dma(o_sffn[l].rearrange("c p t -> p c t"), stF, reads=["Hs"], is_out=True)

              stage(8)
              for (c0, w, ti) in nts:
                  rstd_of(nacc[ti], ("nacc", ti), w, 1.0 / D)
                  for k in range(KC):
                      ot, otk = sqrot.get()
                      stt(ot[:, 0:w], x[:, k, c0:c0 + w], cv[:, 2 * CVL + k:2 * CVL + k + 1], rt[:, 0:w], ALU.mult, ALU.mult,
                          [("x", k, ti), "rt", "cv"], [otk])
                      if ti == 0:
                          S.dma(yT[k * 128:(k + 1) * 128, tok0:tok0 + PB], ot[:, 0:w], reads=[otk], is_out=True)
                      else:
                          S.dma(ysT[k * 128:(k + 1) * 128, :], ot[:, 0:w], reads=[otk], is_out=True)

          except _Stop:
            pass

        S.emit()
    return nc


def _pc(v, n):
    return np.ascontiguousarray(np.asarray(v, np.float32).reshape(n, 128).T)


def _perm_in():
    idx = list(range(6144, 6176))
    for g in range(8):
        idx += list(range(8224 + 256 * g, 8224 + 256 * (g + 1)))
        idx += list(range(6176 + 256 * g, 6176 + 256 * (g + 1)))
    for g in range(8):
        idx += list(range(2048 + 256 * g, 2048 + 256 * (g + 1)))
        idx += list(range(4096 + 128 * g, 4096 + 128 * (g + 1)))
        idx += list(range(5120 + 128 * g, 5120 + 128 * (g + 1)))
        idx += list(range(256 * g, 256 * (g + 1)))
    return np.array(idx)


def _perm_up():
    idx = []
    for s in range(4):
        for j in range(11):
            cg = 11 * s + j
            cvv = 44 + 11 * s + j
            idx += list(range(128 * cg, 128 * (cg + 1)))
            idx += list(range(128 * cvv, 128 * (cvv + 1)))
    return np.array(idx)


_NC_CACHE = {}


def kernel(x_prompt, x_sample, state_ssd_conv, state_ssd, state_ffn_conv, norm1_w, w_in, ssd_conv_w,
           ssd_conv_b, dt_bias, a_log, ssd_d, ssd_norm_w, gmlp_norm_w, gmlp_w_s, gmlp_b_s, w_out,
           norm2_w, w_up, ffn_conv_w, ffn_conv_b, w_down, final_norm_w):
    f = lambda a: np.asarray(a, np.float32)
    x_prompt, x_sample = f(x_prompt), f(x_sample)
    state_ssd_conv, state_ssd, state_ffn_conv = f(state_ssd_conv), f(state_ssd), f(state_ffn_conv)
    n = 8
    cvec = np.zeros((128, 2 * CVL + 16), np.float32)
    for l in range(NL):
        b = l * CVL
        cvec[:, b + 0:b + 16] = _pc(norm1_w[l], 16)
        cvec[:, b + 16:b + 32] = _pc(norm2_w[l], 16)
        cvec[:, b + 32:b + 48] = _pc(ssd_norm_w[l], 16)
        cvec[:, b + 48:b + 64] = _pc(gmlp_norm_w[l], 16)
        cvec[:, b + 64:b + 80] = _pc(np.repeat(f(ssd_d[l]), 64), 16)
        cvec[:, b + 80:b + 112] = _pc(ssd_conv_b[l], 32)
        for k in range(4):
            cvec[:, b + 112 + 32 * k:b + 112 + 32 * (k + 1)] = _pc(f(ssd_conv_w[l])[k], 32)
        cvec[:, b + 240:b + 328] = _pc(ffn_conv_b[l], 88)
        for k in range(3):
            cvec[:, b + 328 + 88 * k:b + 328 + 88 * (k + 1)] = _pc(f(ffn_conv_w[l])[k], 88)
    cvec[:, 2 * CVL:2 * CVL + 16] = _pc(final_norm_w, 16)
    alog = np.ascontiguousarray(f(a_log))
    dtb = np.ascontiguousarray(f(dt_bias).T)
    bs = np.ascontiguousarray(f(gmlp_b_s).reshape(NL, 1024))
    bss = np.ascontiguousarray(f(gmlp_b_s)[:, :, 0:16].reshape(NL, 128))
    wsT = np.ascontiguousarray(f(gmlp_w_s).transpose(0, 3, 1, 2).reshape(NL, 128, 1024))
    w_in_r = np.ascontiguousarray(f(w_in)[:, :, _perm_in()])
    w_up_r = np.ascontiguousarray(f(w_up)[:, :, _perm_up()])
    w_out_c = np.ascontiguousarray(f(w_out))
    w_down_c = np.ascontiguousarray(f(w_down))
    if MK_SMALLW:
        w_up_r = w_out_c = w_down_c = np.zeros((NL, 128, 128), np.float32)
    shared = dict(cvec=cvec, alog=alog, dtb=dtb, bs=bs, bss=bss, wsT=wsT, w_in=w_in_r, w_out=w_out_c, w_up=w_up_r,
                  w_down=w_down_c)
    in_maps = []
    xTs = [np.ascontiguousarray(x_prompt[b].T) for b in range(4)]
    for c in range(n):
        sq = slice(NSQ * c, NSQ * (c + 1))
        m = dict(shared)
        m["xT"] = xTs[c % 4]
        m["xsT"] = np.ascontiguousarray(x_sample[sq].reshape(NSAMP, D).T)
        m["sconv"] = np.ascontiguousarray(
            state_ssd_conv[:, sq].reshape(NL, NSQ, 3, 32, 128).transpose(0, 3, 4, 1, 2).reshape(NL, 32, 128, NSQ * 3))
        m["sssd"] = np.ascontiguousarray(state_ssd[:, sq].reshape(NL, NSQ, 2048, 128).transpose(0, 1, 3, 2))
        m["sffn"] = np.ascontiguousarray(
            state_ffn_conv[:, sq].reshape(NL, NSQ, 2, 88, 128).transpose(0, 3, 4, 1, 2).reshape(NL, 88, 128, NSQ * 2))
        in_maps.append(m)
    if "nc" not in _NC_CACHE:
        _NC_CACHE["nc"] = build_program()
    nc = _NC_CACHE["nc"]
    if MK_CORES < n:
        in_maps = in_maps[:MK_CORES]
    res = run_bass_kernel_spmd(nc, in_maps, core_ids=list(range(len(in_maps))))
    R = list(res.results) + [res.results[0]] * (n - len(in_maps))
    y_prompt = np.stack([R[b]["yT"].T for b in range(4)]).astype(np.float32)
    y_sample = np.concatenate([R[c]["ysT"].T.reshape(NSQ, ST, D) for c in range(n)]).astype(np.float32)
    p_conv = np.stack([R[b]["o_pconv"].transpose(0, 3, 1, 2).reshape(NL, 3, 4096) for b in range(4)], axis=1)
    p_ssd = np.stack([R[b]["o_pssd"].transpose(0, 2, 1).reshape(NL, 32, 64, 128) for b in range(4)], axis=1)
    p_ffn = np.stack([R[b]["o_pffn"].transpose(0, 3, 1, 2).reshape(NL, 2, 2 * DFF) for b in range(4)], axis=1)
    s_conv = np.concatenate(
        [R[c]["o_sconv"].reshape(NL, 32, 128, NSQ, 3).transpose(0, 3, 4, 1, 2).reshape(NL, NSQ, 3, 4096) for c in range(n)],
        axis=1)
    s_ssd = np.concatenate([R[c]["o_sssd"].transpose(0, 1, 3, 2).reshape(NL, NSQ, 32, 64, 128) for c in range(n)], axis=1)
    s_ffn = np.concatenate(
        [R[c]["o_sffn"].reshape(NL, 88, 128, NSQ, 2).transpose(0, 3, 4, 1, 2).reshape(NL, NSQ, 2, 2 * DFF) for c in range(n)],
        axis=1)
    s_v = np.concatenate(
        [R[c]["o_sv"].reshape(NL, 16, 128, NSQ, ST).transpose(0, 3, 4, 1, 2).reshape(NL, NSQ, ST, 2048) for c in range(n)],
        axis=1)
    c32 = lambda a: np.ascontiguousarray(a, dtype=np.float32)
    return (c32(y_prompt), c32(y_sample), c32(p_conv), c32(p_ssd), c32(p_ffn), c32(s_conv), c32(s_ssd), c32(s_ffn),
            c32(s_v))
```
